# Optimizing a Trainium2 kernel written in Bass

```python
import math
import jax, jax.numpy as jnp
from jax import lax
import numpy as np

D_MODEL = 2048
BATCH = 1
SEQ = 16384
DEPTH = 1

HEAD_DIM = 64
D_MIX = D_MODEL
SB_HEADS = 16
SW_HEADS = 16
SW_KV_HEADS = 4
SB_WIDTH = SB_HEADS * HEAD_DIM
SW_WIDTH = SW_HEADS * HEAD_DIM
SW_KV_WIDTH = SW_KV_HEADS * HEAD_DIM
WINDOW = 128
BLOCK = 128
N_BUCKETS = 32
MAX_DISTANCE = 128
RMS_EPS = 1e-6
SPLIT_POINTS = (
    SB_WIDTH,
    2 * SB_WIDTH,
    3 * SB_WIDTH,
    4 * SB_WIDTH,
    4 * SB_WIDTH + SW_WIDTH,
    4 * SB_WIDTH + SW_WIDTH + SW_KV_WIDTH,
    4 * SB_WIDTH + SW_WIDTH + 2 * SW_KV_WIDTH,
)
D_IN_PROJ = 4 * SB_WIDTH + 2 * SW_WIDTH + 2 * SW_KV_WIDTH

kernel_name = "hymba_stickbreak_swa_sandwich"


def rmsnorm(x, g):
    xf = x.astype(jnp.float32)
    y = xf * lax.rsqrt(jnp.mean(xf * xf, axis=-1, keepdims=True) + RMS_EPS)
    return (y * g.astype(jnp.float32)).astype(x.dtype)


def t5_bucket(dist):
    n = jnp.maximum(dist, 0)
    max_exact = N_BUCKETS // 2
    nf = jnp.maximum(n, 1).astype(jnp.float32)
    large = max_exact + (jnp.log(nf / max_exact) / math.log(MAX_DISTANCE / max_exact)
                         * (N_BUCKETS - max_exact)).astype(jnp.int32)
    large = jnp.minimum(large, N_BUCKETS - 1)
    return jnp.where(n < max_exact, n, large)


def stick_breaking_attention(q, k, v):
    b, s = q.shape[:2]
    nb = s // BLOCK
    scale = HEAD_DIM ** -0.5
    qh = q.reshape(b, nb, BLOCK, SB_HEADS, HEAD_DIM).transpose(1, 0, 2, 3, 4)
    kh = k.reshape(b, s, SB_HEADS, HEAD_DIM)
    vh = v.reshape(b, s, SB_HEADS, HEAD_DIM).astype(jnp.float32)
    key_pos = jnp.arange(s)

    def one_block(args):
        q_blk, blk = args
        z = jnp.einsum('bqhd,bshd->bhqs', q_blk, kh).astype(jnp.float32) * scale
        q_pos = blk * BLOCK + jnp.arange(BLOCK)
        before = key_pos[None, :] < q_pos[:, None]
        log_1m_beta = jnp.where(before, jax.nn.log_sigmoid(-z), 0.0)
        suffix = lax.cumsum(log_1m_beta, axis=3, reverse=True) - log_1m_beta
        w = jnp.where(before, jnp.exp(jax.nn.log_sigmoid(z) + suffix), 0.0)
        return jnp.einsum('bhqs,bshd->bqhd', w, vh)

    out = lax.map(one_block, (qh, jnp.arange(nb)))
    return out.transpose(1, 0, 2, 3, 4).reshape(b, s, SB_WIDTH).astype(q.dtype)


def sliding_window_gqa(q, k, v, sinks, rel_bias):
    b, s = q.shape[:2]
    nb = s // BLOCK
    grp = SW_HEADS // SW_KV_HEADS
    scale = HEAD_DIM ** -0.5
    qb = q.reshape(b, nb, BLOCK, SW_KV_HEADS, grp, HEAD_DIM)
    kh = k.reshape(b, s, SW_KV_HEADS, HEAD_DIM)
    vh = v.reshape(b, s, SW_KV_HEADS, HEAD_DIM)
    pad = ((0, 0), (BLOCK, 0), (0, 0), (0, 0))
    kp = jnp.pad(kh, pad)
    vp = jnp.pad(vh, pad)
    kb = jnp.concatenate([kp[:, :s].reshape(b, nb, BLOCK, SW_KV_HEADS, HEAD_DIM),
                          kh.reshape(b, nb, BLOCK, SW_KV_HEADS, HEAD_DIM)], axis=2)
    vb = jnp.concatenate([vp[:, :s].reshape(b, nb, BLOCK, SW_KV_HEADS, HEAD_DIM),
                          vh.reshape(b, nb, BLOCK, SW_KV_HEADS, HEAD_DIM)], axis=2)
    logits = jnp.einsum('bnqhgd,bnchd->bnhgqc', qb, kb).astype(jnp.float32) * scale

    qi = jnp.arange(BLOCK)[:, None]
    ci = jnp.arange(2 * BLOCK)[None, :]
    dist = qi + BLOCK - ci
    bias = rel_bias.astype(jnp.float32)[t5_bucket(dist)]
    bias = bias.transpose(2, 0, 1).reshape(SW_KV_HEADS, grp, BLOCK, 2 * BLOCK)
    in_window = (dist >= 0) & (dist < WINDOW)
    key_pos = jnp.arange(nb)[:, None] * BLOCK - BLOCK + ci
    valid = in_window[None] & (key_pos >= 0)[:, None, :]
    logits = jnp.where(valid[None, :, None, None], logits + bias, -jnp.inf)

    sink = sinks.astype(jnp.float32).reshape(SW_KV_HEADS, grp)[None, None, :, :, None, None]
    m = jnp.maximum(jnp.max(logits, axis=-1, keepdims=True), sink)
    p = jnp.exp(logits - m)
    denom = jnp.sum(p, axis=-1, keepdims=True) + jnp.exp(sink - m)
    out = jnp.einsum('bnhgqc,bnchd->bnqhgd', p / denom, vb.astype(jnp.float32))
    return out.reshape(b, s, SW_WIDTH).astype(q.dtype)


def setup_inputs(seed: int = 0) -> dict:
    key = jax.random.key(seed)
    ks = jax.random.split(key, 9)
    x = jax.random.normal(ks[0], (BATCH, SEQ, D_MODEL), jnp.float32)
    w_in = jax.random.normal(ks[1], (DEPTH, D_MODEL, D_IN_PROJ), jnp.float32) * D_MODEL ** -0.5
    w_out = jax.random.normal(ks[2], (DEPTH, D_MIX, D_MODEL), jnp.float32) * D_MIX ** -0.5
    norm_pre = 1.0 + 0.02 * jax.random.normal(ks[3], (DEPTH, D_MODEL), jnp.float32)
    norm_post = 1.0 + 0.02 * jax.random.normal(ks[4], (DEPTH, D_MODEL), jnp.float32)
    gn_sb = 1.0 + 0.02 * jax.random.normal(ks[5], (DEPTH, SB_WIDTH), jnp.float32)
    gn_sw = 1.0 + 0.02 * jax.random.normal(ks[6], (DEPTH, SW_WIDTH), jnp.float32)
    sinks = 0.5 * jax.random.normal(ks[7], (DEPTH, SW_HEADS), jnp.float32)
    rel_bias = 0.1 * jax.random.normal(ks[8], (N_BUCKETS, SW_HEADS), jnp.float32)
    return {"x": x, "w_in": w_in, "w_out": w_out, "norm_pre": norm_pre, "norm_post": norm_post,
            "gn_sb": gn_sb, "gn_sw": gn_sw, "sinks": sinks, "rel_bias": rel_bias}


def reference(x, w_in, w_out, norm_pre, norm_post, gn_sb, gn_sw, sinks, rel_bias):
    h = x
    for l in range(DEPTH):
        u = rmsnorm(h, norm_pre[l])
        proj = jnp.einsum('bsd,de->bse', u, w_in[l])
        q_sb, k_sb, v_sb, z_sb, q_sw, k_sw, v_sw, z_sw = jnp.split(proj, SPLIT_POINTS, axis=-1)
        o_sb = stick_breaking_attention(q_sb, k_sb, v_sb)
        o_sw = sliding_window_gqa(q_sw, k_sw, v_sw, sinks[l], rel_bias)
        o_sb = rmsnorm(o_sb, gn_sb[l]) * jax.nn.silu(z_sb)
        o_sw = rmsnorm(o_sw, gn_sw[l]) * jax.nn.silu(z_sw)
        y = jnp.einsum('bse,ed->bsd', jnp.concatenate([o_sb, o_sw], axis=-1), w_out[l])
        h = h + rmsnorm(y, norm_post[l])
    return h
```

```python
import numpy as np
import ml_dtypes
from contextlib import ExitStack
import concourse.bass as bass
import concourse.mybir as mybir
from concourse.bass_utils import run_bass_kernel_spmd

F32 = mybir.dt.float32
BF16 = mybir.dt.bfloat16
AF = mybir.ActivationFunctionType
ALU = mybir.AluOpType
AX = mybir.AxisListType

D = 2048
DIN = 6656
NCORES = 8
NEG = -30000.0
EPS = 1e-6


class Buf:
    def __init__(self, name):
        self.name = name
        self.writers = []
        self.readers = []
        self.prev = set()
        self.gen_open = False


class Op:
    __slots__ = ("eng", "fn", "deps", "idx", "key", "token", "need")

    def __init__(self, eng, fn, deps, idx, key):
        self.eng, self.fn, self.deps, self.idx, self.key = eng, fn, deps, idx, key
        self.token = None
        self.need = False


class Sched:
    ENGS = ("pe", "act", "dve", "pool", "sp")

    def __init__(self):
        self.ops = []

    def add(self, eng, fn, reads=(), writes=(), pwrites=(), key=None):
        idx = len(self.ops)
        deps = set()
        for b in reads:
            deps.update(b.writers)
        for b in writes:
            prev = set(b.writers) | set(b.readers)
            deps.update(prev)
            b.prev, b.writers, b.readers, b.gen_open = prev, [idx], [], False
        for b in pwrites:
            if b.gen_open and not b.readers:
                deps.update(b.prev)
                b.writers.append(idx)
            else:
                prev = set(b.writers) | set(b.readers)
                deps.update(prev)
                b.prev, b.writers, b.readers, b.gen_open = prev, [idx], [], True
        for b in reads:
            b.readers.append(idx)
        deps.discard(idx)
        self.ops.append(Op(eng, fn, deps, idx, key))
        return idx

    def emit(self, nc, stack):
        ops = self.ops
        for op in ops:
            for d in op.deps:
                dop = ops[d]
                if dop.eng == "pe" and op.eng == "pe" and dop.key is None and op.key is None:
                    continue
                dop.need = True
        sems = {}

        def sem(name):
            if name not in sems:
                sems[name] = stack.enter_context(nc.semaphore("s_" + name))
            return sems[name]

        counts = {}
        for op in ops:
            if op.key is not None:
                k = "d_" + op.key
                counts[k] = counts.get(k, 0) + 16
                op.token = (k, counts[k])
            elif op.need:
                k = "e_" + op.eng
                counts[k] = counts.get(k, 0) + 1
                op.token = (k, counts[k])
        per = {e: [] for e in self.ENGS}
        for op in ops:
            per[op.eng].append(op)
        block = stack.enter_context(nc.Block())

        def body(engname):
            def run(eng):
                waited = {}
                for op in per[engname]:
                    need = {}
                    for d in op.deps:
                        dop = ops[d]
                        if dop.token is None:
                            continue
                        if dop.eng == "pe" and engname == "pe" and dop.key is None and op.key is None:
                            continue
                        k, v = dop.token
                        if need.get(k, 0) < v:
                            need[k] = v
                    for k, v in need.items():
                        if waited.get(k, 0) >= v:
                            continue
                        eng.wait_ge(sem(k), v)
                        waited[k] = v
                    ins = op.fn(eng)
                    if op.token is not None:
                        k, v = op.token
                        ins.then_inc(sem(k), 16 if op.key is not None else 1)
                if engname == "sp":
                    for k, v in counts.items():
                        if waited.get(k, 0) < v:
                            eng.wait_ge(sem(k), v)
            return run

        block.tensor(body("pe"))
        block.scalar(body("act"))
        block.vector(body("dve"))
        block.gpsimd(body("pool"))
        block.sync(body("sp"))


def build_nc(NS):
    NB = 8 * NS
    nc = bass.Bass("TRN2", target_bir_lowering=False)
    S = Sched()
    st = ExitStack()

    def dram_in(name, shape, dt=F32):
        return nc.dram_tensor(name, list(shape), dt, kind="ExternalInput").ap()

    xs = dram_in("xs", [NB * 128, D])
    w_in = dram_in("w_in", [D, DIN])
    w_out = dram_in("w_out", [D, D])
    npre = dram_in("npre", [128, 16])
    gsbw = dram_in("gsbw", [128, 16])
    npost = dram_in("npost", [128, D])
    sinkb = dram_in("sinkb", [128, 16])
    biasT = dram_in("biasT", [128, 16 * 256])
    maskc = dram_in("maskc", [128, 256])
    mask0 = dram_in("mask0", [128, 256])
    ident_d = dram_in("ident", [128, 128], BF16)
    tri_d = dram_in("tri", [128, 128], BF16)
    erow_d = dram_in("erow", [128, 256], BF16)
    dmask_d = dram_in("dmask", [128, 1024], BF16)
    out = nc.dram_tensor("out", [NS * 128, D], F32, kind="ExternalOutput").ap()

    KT_d = nc.dram_tensor("KT_d", [128, NB, 8, 128], BF16).ap()
    V_d = nc.dram_tensor("V_d", [128, NB, 1024], BF16).ap()
    QT_d = nc.dram_tensor("QT_d", [128, NS, 8, 128], BF16).ap()
    SG_d = nc.dram_tensor("SG_d", [128, NS, 1024], BF16).ap()
    OGW_d = nc.dram_tensor("OGW_d", [128, NS, 1024], BF16).ap()
    B_KT = [Buf("KT_d%d" % i) for i in range(NB)]
    B_V = [Buf("V_d%d" % i) for i in range(NB)]
    B_QT = [Buf("QT_d%d" % j) for j in range(NS)]
    B_SG = [Buf("SG_d%d" % j) for j in range(NS)]
    B_OGW = [Buf("OGW_d%d" % j) for j in range(NS)]

    ARENA_BYTES = 204 * 1024
    arena = st.enter_context(nc.sbuf_tensor("arena", [128, ARENA_BYTES // 2], BF16))
    apos = [0]

    class _T:
        def __init__(self, ap):
            self.ap = ap

        def __getitem__(self, key):
            return self.ap[key]

    def sb(name, shape, dt):
        nfree = 1
        for s_ in shape[1:]:
            nfree *= s_
        nbytes = nfree * (4 if dt == F32 else 2)
        nbytes = (nbytes + 63) // 64 * 64
        off = apos[0]
        apos[0] += nbytes
        assert apos[0] <= ARENA_BYTES, (name, apos[0])
        v = arena[:, off // 2:(off + nbytes) // 2]
        if dt == F32:
            v = v.bitcast(F32)
        v = v[:, 0:nfree]
        if len(shape) == 3:
            v = v.rearrange("p (a b) -> p a b", a=shape[1])
        elif len(shape) == 4:
            v = v.rearrange("p (a b c) -> p a b c", a=shape[1], b=shape[2])
        return _T(v)

    ps = st.enter_context(nc.psum_tensor("ps", [128, 8 * 512], F32))
    B_ps = [Buf("bank%d" % k) for k in range(8)]

    def psf(b0, ncols):
        return ps[:, b0 * 512: b0 * 512 + ncols]

    def psb(b0, ncols):
        return ps[:, b0 * 512: b0 * 512 + ncols // 2].bitcast(BF16)

    ident = sb("ident", [128, 128], BF16)
    tri = sb("tri", [128, 128], BF16)
    erow = sb("erow", [128, 256], BF16)
    dmask = sb("dmask", [128, 1024], BF16)
    npre_s = sb("npre_s", [128, 16], F32)
    gsbw_s = sb("gsbw_s", [128, 16], F32)
    B_const = Buf("const")

    def ld_const(dst, src):
        S.add("sp", lambda e: e.dma_start(out=dst, in_=src), pwrites=[B_const], key="const")
    ld_const(ident[:], ident_d)
    ld_const(tri[:], tri_d)
    ld_const(erow[:], erow_d)
    ld_const(dmask[:], dmask_d)
    ld_const(npre_s[:], npre)
    ld_const(gsbw_s[:], gsbw)

    xt = [sb("xt%d" % k, [128, D], F32) for k in range(2)]
    B_xt = [Buf("xt%d" % k) for k in range(2)]
    xjunk = sb("xjunk", [128, D], BF16)
    B_xjunk = Buf("xjunk")
    xb = [sb("xb%d" % k, [128, D], BF16) for k in range(2)]
    B_xb = [Buf("xb%d" % k) for k in range(2)]
    uT = [sb("uT%d" % k, [128, 16, 128], BF16) for k in range(2)]
    B_uT = [Buf("uT%d" % k) for k in range(2)]
    stat = sb("stat", [128, 8], F32)
    B_stat = Buf("stat")
    wst = [sb("wst%d" % k, [128, 2048], F32) for k in range(2)]
    B_wst = [Buf("wst%d" % k) for k in range(2)]

    def load_weights(W, B_W, colranges, scale_ap, nchunks=16, src=None):
        src = w_in if src is None else src
        n = 0
        for c in range(nchunks):
            off = 0
            for (c0, c1) in colranges:
                w = c1 - c0
                k = n % 2
                n += 1
                S.add("sp", lambda e, k=k, c=c, c0=c0, c1=c1, w=w: e.dma_start(
                    out=wst[k][:, 0:w], in_=src[c * 128:(c + 1) * 128, c0:c1]),
                    writes=[B_wst[k]], key="wst%d" % k)
                S.add("dve", lambda e, k=k, c=c, off=off, w=w: e.tensor_scalar(
                    W[:, c, off:off + w], wst[k][:, 0:w], scale_ap[:, c:c + 1], None, ALU.mult),
                    reads=[B_wst[k], B_const], pwrites=[B_W])
                off += w

    def norm_transpose(i, cnt, tpb=(0, 0)):
        k = cnt % 2
        tb = tpb[k]
        S.add("sp", lambda e: e.dma_start(out=xt[k][:], in_=xs[i * 128:(i + 1) * 128, :]),
              writes=[B_xt[k]], key="xt%d" % k)
        S.add("act", lambda e: e.activation(xjunk[:], xt[k][:], AF.Square, accum_out=stat[:, 0:1]),
              reads=[B_xt[k]], writes=[B_xjunk, B_stat])
        rstd_chain(stat[:, 0:1], stat[:, 1:2], stat[:, 2:3], 1.0 / D, B_stat)
        S.add("dve", lambda e: e.tensor_scalar(xb[k][:], xt[k][:], stat[:, 2:3], None, ALU.mult),
              reads=[B_xt[k], B_stat], writes=[B_xb[k]])

        def tr(e):
            ins = None
            for c in range(16):
                ins = e.transpose(psb(tb, 2048)[:, c * 128:(c + 1) * 128], xb[k][:, c * 128:(c + 1) * 128], ident[:])
            return ins
        S.add("pe", tr, reads=[B_xb[k], B_const], writes=[B_ps[tb], B_ps[tb + 1]])
        S.add("act", lambda e: e.activation(uT[k][:].rearrange("p c t -> p (c t)"), psb(tb, 2048), AF.Copy),
              reads=[B_ps[tb], B_ps[tb + 1]], writes=[B_uT[k]])
        return k

    def rstd_chain(ss, tmp, rstd, inv_n, B):
        S.add("dve", lambda e: e.tensor_scalar(tmp, ss, inv_n, EPS, ALU.mult, ALU.add), reads=[B], writes=[B])
        S.add("act", lambda e: e.activation(tmp, tmp, AF.Ln), reads=[B], writes=[B])
        S.add("act", lambda e: e.activation(rstd, tmp, AF.Exp, scale=-0.5), reads=[B], writes=[B])

    def mm_feat(bank_ap, W, col0, n_m, rhs_of_c, start_first=True):
        def f(e):
            ins = None
            for c in range(16):
                ins = e.matmul(bank_ap, W[:, c, col0:col0 + n_m], rhs_of_c(c),
                               start=(c == 0 and start_first), stop=(c == 15), skip_group_check=True)
            return ins
        return f

    def tm_project(W, B_W, col0, k):
        for hf in range(2):
            def f(e, hf=hf):
                ins = None
                for c in range(16):
                    ins = e.matmul(psf(2 + hf, 512), uT[k][:, c, :], W[:, c, col0 + hf * 512:col0 + (hf + 1) * 512],
                                   start=(c == 0), stop=(c == 15))
                return ins
            S.add("pe", f, reads=[B_W, B_uT[k]], writes=[B_ps[2 + hf]])

    def tm_to_feat(tm, B_tm, tb, dst, B_dst, scale):
        S.add("dve", lambda e: e.tensor_scalar(tm[:], psf(2, 1024), scale, None, ALU.mult),
              reads=[B_ps[2], B_ps[3]], writes=[B_tm])

        def ftr(e):
            ins = None
            for pr in range(8):
                ins = e.transpose(psb(tb, 1024)[:, pr * 128:(pr + 1) * 128], tm[:, pr * 128:(pr + 1) * 128], ident[:])
            return ins
        S.add("pe", ftr, reads=[B_tm, B_const], writes=[B_ps[tb]])
        S.add("act", lambda e: e.activation(dst[:].rearrange("p a t -> p (a t)"), psb(tb, 1024), AF.Copy),
              reads=[B_ps[tb]], writes=[B_dst])

    mark = apos[0]
    if True:
        Wkv = sb("Wkv", [128, 16, 2048], BF16)
        B_Wkv = Buf("Wkv")
        kst = [sb("kst%d" % k, [128, 8, 128], BF16) for k in range(2)]
        vst = [sb("vst%d" % k, [128, 1024], BF16) for k in range(2)]
        B_kst = [Buf("kst%d" % k) for k in range(2)]
        B_vst = [Buf("vst%d" % k) for k in range(2)]
        load_weights(Wkv, B_Wkv, [(1024, 3072)], npre_s)
        Wq = sb("Wq", [128, 16, 2048], BF16)
        B_Wq = Buf("Wq")
        qst = sb("qst", [128, 8, 128], BF16)
        B_qst = Buf("qst")
        ge = sb("ge", [128, 1024], F32)
        B_ge = Buf("ge")
        gst = sb("gst", [128, 1024], BF16)
        B_gst = Buf("gst")
        load_weights(Wq, B_Wq, [(0, 1024), (3072, 4096)], npre_s)
        ktm = sb("ktm", [128, 1024], BF16)
        B_ktm = Buf("ktm")
        qtm = sb("qtm", [128, 1024], BF16)
        B_qtm = Buf("qtm")
        norm_transpose(0, 0, (0, 6))
        for i in range(NB):
            k = i % 2
            if i + 1 < NB:
                norm_transpose(i + 1, i + 1, (0, 6))
            tb = (0, 6)[k]
            tm_project(Wkv, B_Wkv, 0, k)
            for hf in range(2):
                def fv(e, hf=hf, k=k):
                    ins = None
                    for c in range(16):
                        ins = e.matmul(psf(4 + hf, 512), uT[k][:, c, :], Wkv[:, c, 1024 + hf * 512:1024 + (hf + 1) * 512],
                                       start=(c == 0), stop=(c == 15))
                    return ins
                S.add("pe", fv, reads=[B_Wkv, B_uT[k]], writes=[B_ps[4 + hf]])
            S.add("dve", lambda e, k=k: e.tensor_copy(vst[k][:], psf(4, 1024)),
                  reads=[B_ps[4], B_ps[5]], writes=[B_vst[k]])
            S.add("pool", lambda e, k=k, i=i: e.dma_start(out=V_d[:, i, :], in_=vst[k][:]),
                  reads=[B_vst[k]], writes=[B_V[i]], key="vst%d" % k)
            tm_to_feat(ktm, B_ktm, tb, kst[k], B_kst[k], 1.0)
            S.add("pool", lambda e, k=k, i=i: e.dma_start(out=KT_d[:, i, :, :], in_=kst[k][:]),
                  reads=[B_kst[k]], writes=[B_KT[i]], key="kst%d" % k)
            if i % 8 == 7:
                j = i // 8
                tm_project(Wq, B_Wq, 0, k)
                for hf in range(2):
                    def fg(e, hf=hf, k=k):
                        ins = None
                        for c in range(16):
                            ins = e.matmul(psf(4 + hf, 512), uT[k][:, c, :], Wq[:, c, 1024 + hf * 512:1024 + (hf + 1) * 512],
                                           start=(c == 0), stop=(c == 15))
                        return ins
                    S.add("pe", fg, reads=[B_Wq, B_uT[k]], writes=[B_ps[4 + hf]])
                silu_from_psum(S, psf(4, 1024), [B_ps[4], B_ps[5]], ge, B_ge, gst, B_gst)
                S.add("pool", lambda e, j=j: e.dma_start(out=SG_d[:, j, :], in_=gst[:]),
                      reads=[B_gst], writes=[B_SG[j]], key="gst")
                tm_to_feat(qtm, B_qtm, tb, qst, B_qst, 0.125)
                S.add("pool", lambda e, j=j: e.dma_start(out=QT_d[:, j, :, :], in_=qst[:]),
                      reads=[B_qst], writes=[B_QT[j]], key="qst")
        p1_tail = [B_Wkv] + B_kst + B_vst + [B_Wq, B_qst, B_ge, B_gst, B_ktm, B_qtm]
    B_region = Buf("region")
    S.add("sp", lambda e: e.nop(), writes=p1_tail + [B_region])

    apos[0] = mark
    if True:
        W3 = sb("W3", [128, 16, 2560], BF16)
        B_W3 = Buf("W3")
        bias_s = sb("bias_s", [128, 16, 256], F32)
        mtmp = sb("mtmp", [128, 2, 256], F32)
        sink_s = sb("sink_s", [128, 16], F32)
        B_bias = Buf("bias")
        Ksw = sb("Ksw", [128, 4, 256], BF16)
        Vsw = sb("Vsw", [128, 2, 256], BF16)
        Qsw = sb("Qsw", [128, 8, 128], BF16)
        lg = sb("lg", [128, 4, 256], F32)
        pexp = sb("pexp", [128, 4, 256], BF16)
        pT = sb("pT", [128, 4, 2, 128], BF16)
        sm = sb("sm", [128, 32], F32)
        osw = sb("osw", [128, 1024], F32)
        ge3 = sb("ge3", [128, 1024], F32)
        sg3 = sb("sg3", [128, 1024], BF16)
        og3 = sb("og3", [128, 1024], BF16)
        B_Ksw, B_Vsw, B_Qsw, B_lg, B_pexp, B_pT, B_sm, B_osw, B_ge3, B_sg3, B_og3 = [
            Buf(n) for n in ("Ksw", "Vsw", "Qsw", "lg", "pexp", "pT", "sm", "osw", "ge3", "sg3", "og3")]
        S.add("sp", lambda e: e.nop(), reads=[B_region],
              writes=[B_W3, B_bias, B_Ksw, B_Vsw, B_Qsw, B_lg, B_pexp, B_pT, B_sm, B_osw, B_ge3, B_sg3, B_og3])
        load_weights(W3, B_W3, [(4096, 5120), (5120, 5376), (5376, 5632), (5632, 6656)], npre_s)
        S.add("sp", lambda e: e.dma_start(out=bias_s[:].rearrange("p h c -> p (h c)"), in_=biasT), pwrites=[B_bias], key="bias")
        S.add("sp", lambda e: e.dma_start(out=mtmp[:, 0, :], in_=maskc), pwrites=[B_bias], key="bias")
        S.add("sp", lambda e: e.dma_start(out=mtmp[:, 1, :], in_=mask0), pwrites=[B_bias], key="bias")
        S.add("sp", lambda e: e.dma_start(out=sink_s[:], in_=sinkb), pwrites=[B_bias], key="bias")

        def bias_setup(e):
            ins = None
            for h in range(16):
                ins = e.tensor_tensor(bias_s[:, h, :], bias_s[:, h, :], mtmp[:, 0, :], ALU.add)
            return ins
        S.add("pool", bias_setup, reads=[B_bias], writes=[B_bias])

        cnt = 0
        for j in range(NS):
            for which, i in ((0, 8 * j + 6), (1, 8 * j + 7)):
                k = norm_transpose(i, cnt)
                cnt += 1
                for g in range(4):
                    for half in range(2):
                        S.add("pe", mm_feat(psf(2, 512)[half * 64:(half + 1) * 64, g * 128:(g + 1) * 128], W3,
                                            1024 + g * 64, 64, lambda c, k=k: uT[k][:, c, :],
                                            start_first=(g == 0)),
                              reads=[B_W3, B_uT[k]], pwrites=[B_ps[2]])
                S.add("act", lambda e, which=which: e.activation(
                    Ksw[:, :, which * 128:(which + 1) * 128], psf(2, 512).rearrange("p (g t) -> p g t", g=4), AF.Copy),
                    reads=[B_ps[2]], pwrites=[B_Ksw])

                def fv3(e, k=k):
                    ins = None
                    for c in range(16):
                        ins = e.matmul(psf(3, 256), uT[k][:, c, :], W3[:, c, 1280:1536], start=(c == 0), stop=(c == 15))
                    return ins
                S.add("pe", fv3, reads=[B_W3, B_uT[k]], writes=[B_ps[3]])
                S.add("dve", lambda e, which=which: e.tensor_copy(Vsw[:, which, :], psf(3, 256)),
                      reads=[B_ps[3]], pwrites=[B_Vsw])
            for pr in range(8):
                bk = 4 + pr // 4
                S.add("pe", mm_feat(psf(4, 1024)[:, pr * 128:(pr + 1) * 128], W3, pr * 128, 128,
                                    lambda c, k=k: uT[k][:, c, :], start_first=(pr % 4 == 0)),
                      reads=[B_W3, B_uT[k]], pwrites=[B_ps[bk]])
            S.add("act", lambda e: e.activation(Qsw[:].rearrange("p a t -> p (a t)"), psf(4, 1024), AF.Copy, scale=0.125),
                  reads=[B_ps[4], B_ps[5]], writes=[B_Qsw])
            for hf in range(2):
                def fg3(e, hf=hf, k=k):
                    ins = None
                    for c in range(16):
                        ins = e.matmul(psf(6 + hf, 512), uT[k][:, c, :], W3[:, c, 1536 + hf * 512:1536 + (hf + 1) * 512],
                                       start=(c == 0), stop=(c == 15))
                    return ins
                S.add("pe", fg3, reads=[B_W3, B_uT[k]], writes=[B_ps[6 + hf]])
            silu_from_psum(S, psf(6, 1024), [B_ps[6], B_ps[7]], ge3, B_ge3, sg3, B_sg3)
            btab = bias_s
            PERM = (0, 2, 1, 3)
            for g in range(4):
                def flog(e, g=g):
                    ins = None
                    for s_ in range(4):
                        h = 4 * g + PERM[s_]
                        pb = (h % 2) * 64
                        ins = e.matmul(psf(2, 1024)[:, s_ * 256:(s_ + 1) * 256],
                                       Qsw[pb:pb + 64, h // 2, :], Ksw[pb:pb + 64, g, :],
                                       start=(s_ % 2 == 0), stop=True, skip_group_check=True)
                    return ins
                S.add("pe", flog, reads=[B_Qsw, B_Ksw], writes=[B_ps[2], B_ps[3]])
                S.add("dve", lambda e, g=g, btab=btab: e.tensor_tensor(
                    lg[:], psf(2, 1024).rearrange("p (h c) -> p h c", h=4), btab[:, 4 * g:4 * g + 4, :], ALU.add),
                    reads=[B_ps[2], B_ps[3], B_bias], writes=[B_lg])
                if j == 0:
                    def fm0(e):
                        ins = None
                        for s_ in range(4):
                            ins = e.tensor_tensor(lg[:, s_, :], lg[:, s_, :], mtmp[:, 1, :], ALU.add)
                        return ins
                    S.add("dve", fm0, reads=[B_bias], writes=[B_lg])
                S.add("dve", lambda e: e.tensor_reduce(sm[:, 0:4], lg[:], AX.X, ALU.max), reads=[B_lg], writes=[B_sm])
                S.add("dve", lambda e, g=g: e.tensor_tensor(sm[:, 0:4], sm[:, 0:4], sink_s[:, 4 * g:4 * g + 4], ALU.max),
                      reads=[B_sm, B_bias], writes=[B_sm])
                S.add("dve", lambda e: e.tensor_scalar(sm[:, 4:8], sm[:, 0:4], -1.0, None, ALU.mult),
                      reads=[B_sm], writes=[B_sm])
                S.add("dve", lambda e, g=g: e.tensor_tensor(sm[:, 12:16], sink_s[:, 4 * g:4 * g + 4], sm[:, 0:4], ALU.subtract),
                      reads=[B_sm, B_bias], writes=[B_sm])

                def fexp(e):
                    ins = None
                    for hh in range(4):
                        ins = e.activation(pexp[:, hh, :], lg[:, hh, :], AF.Exp, bias=sm[:, 4 + hh:5 + hh],
                                           accum_out=sm[:, 8 + hh:9 + hh])
                    ins = e.activation(sm[:, 12:16], sm[:, 12:16], AF.Exp)
                    return ins
                S.add("act", fexp, reads=[B_lg, B_sm], writes=[B_pexp, B_sm])
                S.add("dve", lambda e: e.tensor_tensor(sm[:, 8:12], sm[:, 8:12], sm[:, 12:16], ALU.add),
                      reads=[B_sm], writes=[B_sm])
                S.add("dve", lambda e: e.reciprocal(sm[:, 16:20], sm[:, 8:12]), reads=[B_sm], writes=[B_sm])

                def ftr(e):
                    ins = None
                    for hh in range(4):
                        for half in range(2):
                            ins = e.transpose(psb(0, 1024)[:, (hh * 2 + half) * 128:(hh * 2 + half + 1) * 128],
                                              pexp[:, hh, half * 128:(half + 1) * 128], ident[:])
                    return ins
                S.add("pe", ftr, reads=[B_pexp, B_const], writes=[B_ps[0]])
                S.add("act", lambda e: e.activation(pT[:].rearrange("p h a t -> p (h a t)"), psb(0, 1024), AF.Copy),
                      reads=[B_ps[0]], writes=[B_pT])

                def fpv(e, g=g):
                    ins = None
                    for hh in range(4):
                        for half in range(2):
                            ins = e.matmul(psf(1, 256)[:, hh * 64:(hh + 1) * 64], pT[:, hh, half, :],
                                           Vsw[:, half, g * 64:(g + 1) * 64],
                                           start=(hh == 0 and half == 0), stop=(half == 1), skip_group_check=True)
                    return ins
                S.add("pe", fpv, reads=[B_pT, B_Vsw], writes=[B_ps[1]])

                def fnorm(e, g=g):
                    ins = None
                    for s_ in range(4):
                        h = 4 * g + PERM[s_]
                        ins = e.tensor_scalar(osw[:, h * 64:(h + 1) * 64], psf(1, 256)[:, s_ * 64:(s_ + 1) * 64],
                                              sm[:, 16 + s_:17 + s_], None, ALU.mult)
                    return ins
                S.add("dve", fnorm, reads=[B_ps[1], B_sm], pwrites=[B_osw])
            S.add("act", lambda e: e.activation(ge3[:], osw[:], AF.Square, accum_out=sm[:, 20:21]),
                  reads=[B_osw], writes=[B_ge3, B_sm])
            rstd_chain(sm[:, 20:21], sm[:, 21:22], sm[:, 22:23], 1.0 / 1024, B_sm)
            S.add("dve", lambda e: e.scalar_tensor_tensor(og3[:], osw[:], sm[:, 22:23], sg3[:], ALU.mult, ALU.mult),
                  reads=[B_osw, B_sm, B_sg3], writes=[B_og3])
            S.add("pool", lambda e, j=j: e.dma_start(out=OGW_d[:, j, :], in_=og3[:]),
                  reads=[B_og3], writes=[B_OGW[j]], key="og3")
        p3_tail = [B_W3, B_bias, B_Ksw, B_Vsw, B_Qsw, B_lg, B_pexp, B_pT, B_sm, B_osw, B_ge3, B_sg3, B_og3]
    S.add("sp", lambda e: e.nop(), writes=p3_tail + B_xt + [B_xjunk, B_stat] + B_xb + B_uT + B_wst + [B_region])

    CH = 2
    apos[0] = mark
    if True:
        sb4 = sb
        Wo = sb4("Wo", [128, 16, 2048], BF16)
        B_Wo = Buf("Wo")
        npost_s = sb4("npost_s", [128, D], F32)
        KA = [sb4("KA%d" % k, [128, CH, 8, 128], BF16) for k in range(2)]
        KB = [sb4("KB%d" % k, [128, CH, 8, 128], BF16) for k in range(2)]
        VC = [sb4("VC%d" % k, [128, CH, 1024], BF16) for k in range(2)]
        B_KA = [Buf("KA%d" % k) for k in range(2)]
        B_KB = [Buf("KB%d" % k) for k in range(2)]
        B_VC = [Buf("VC%d" % k) for k in range(2)]
        QA = sb4("QA", [128, 8, 128], BF16)
        QB = sb4("QB", [128, 8, 128], BF16)
        B_QAd, B_QAr, B_QBd, B_QBr = Buf("QAd"), Buf("QAr"), Buf("QBd"), Buf("QBr")
        eb = [sb4("eb%d" % x, [128, 1024], F32) for x in range(2)]
        spb = [sb4("spb%d" % x, [128, 1024], BF16) for x in range(2)]
        Ab = [sb4("Ab%d" % x, [128, 1024], BF16) for x in range(2)]
        B_eb = [Buf("eb%d" % x) for x in range(2)]
        B_spb = [Buf("spb%d" % x) for x in range(2)]
        B_Ab = [Buf("Ab%d" % x) for x in range(2)]
        osb = sb4("osb", [128, 8, 2, 64], F32)
        B_osb = Buf("osb")
        og = sb4("og", [128, 2048], BF16)
        B_og = Buf("og")
        sgl = sb4("sgl", [128, 1024], BF16)
        B_sgl = Buf("sgl")
        ogT = uT[0]
        B_ogT = Buf("ogT")
        xo = xt[0]
        B_xo = Buf("xo")
        yb = xt[1]
        B_yb = Buf("yb")
        jk = xjunk
        B_jk = Buf("jk")
        st4 = sb4("st4", [128, 8], F32)
        B_st4 = Buf("st4")
        all4 = [B_Wo] + B_KA + B_KB + B_VC + [B_QAd, B_QAr, B_QBd, B_QBr] + B_eb + B_spb + B_Ab + \
            [B_osb, B_og, B_sgl, B_ogT, B_xo, B_yb, B_jk, B_st4]
        S.add("sp", lambda e: e.nop(), reads=[B_region], writes=all4)
        wst4 = wst
        B_wst4 = [Buf("wst4_%d" % k) for k in range(2)]
        for c in range(16):
            k = c % 2
            S.add("sp", lambda e, k=k, c=c: e.dma_start(out=wst4[k][:], in_=w_out[c * 128:(c + 1) * 128, :]),
                  writes=[B_wst4[k]], key="wst4_%d" % k)
            S.add("dve", lambda e, k=k, c=c: e.tensor_scalar(Wo[:, c, :], wst4[k][:], gsbw_s[:, c:c + 1], None, ALU.mult),
                  reads=[B_wst4[k], B_const], pwrites=[B_Wo])
        S.add("sp", lambda e: e.dma_start(out=npost_s[:], in_=npost), pwrites=[B_Wo], key="npost")

        Rq = [sb4("Rq%d" % x, [128, 1024], BF16) for x in range(2)]
        sel = sb4("sel", [128, 2, 128], BF16)

        def kconst(e):
            ins = None
            for k in range(2):
                e.memset(KA[k][64:128], 0.0)
                e.memset(KB[k][0:64], 0.0)
            e.memset(QA[64:128], 0.0)
            e.memset(QB[0:64], 0.0)
            for x in range(2):
                e.memset(Rq[x][:], 0.0)
            e.memset(sel[:], 0.0)
            e.memset(sel[64:65, 0, :], 1.0)
            ins = e.memset(sel[32:33, 1, :], 1.0)
            return ins
        S.add("pool", kconst, writes=B_KA + B_KB + [B_QAr, B_QBr, B_QAd, B_QBd])

        ZB = [[B_ps[0], B_ps[1]], [B_ps[2], B_ps[3]]]
        RB = [B_ps[4], B_ps[5]]
        B_R = [Buf("R_A"), Buf("R_B")]
        OB = [B_ps[6], B_ps[7]]
        Qt = [QA, QB]
        Kt = [KA, KB]
        KR = [65, 128]
        AUG = [64, 32]
        B_Qd = [B_QAd, B_QBd]
        B_Qr = [B_QAr, B_QBr]
        B_Kt = [B_KA, B_KB]
        nchunk = 0

        def z_step(X, kb, bi, first):

            def f(e):
                ins = None
                for h in range(8):
                    ins = e.matmul(psf(2 * X, 1024)[:, h * 128:(h + 1) * 128], Kt[X][kb][:, bi, h, :],
                                   Qt[X][:, h, :],
                                   start=(h % 4 == 0), stop=False, skip_group_check=True)
                if first:
                    for k2 in range(2):
                        ins = e.matmul(psf(2 * X + k2, 512), ident[:], dmask[:, k2 * 512:(k2 + 1) * 512],
                                       start=False, stop=False, skip_group_check=True)
                return ins
            S.add("pe", f, reads=[B_Kt[X][kb], B_Qd[X], B_const], writes=ZB[X])

        for j in range(NS):
            nblk = 8 * j + 8
            S.add("sp", lambda e, j=j: e.dma_start(out=QA[0:64], in_=QT_d[0:64, j, :, :]),
                  reads=[B_QT[j]], writes=[B_QAd], key="QAd")
            S.add("sp", lambda e, j=j: e.dma_start(out=QB[64:128], in_=QT_d[64:128, j, :, :]),
                  reads=[B_QT[j]], writes=[B_QBd], key="QBd")
            blocks = list(range(nblk - 1, -1, -1))
            chunks = [blocks[a:a + CH] for a in range(0, nblk, CH)]

            def load_chunk(ch, kb):
                lo = ch[-1]
                n = len(ch)
                S.add("sp", lambda e: e.dma_start(out=KA[kb][0:64, 0:n], in_=KT_d[0:64, lo:lo + n, :, :]),
                      reads=[B_KT[b] for b in ch], writes=[B_KA[kb]], key="KA%d" % kb)
                S.add("sp", lambda e: e.dma_start(out=KB[kb][64:128, 0:n], in_=KT_d[64:128, lo:lo + n, :, :]),
                      reads=[B_KT[b] for b in ch], writes=[B_KB[kb]], key="KB%d" % kb)
                S.add("sp", lambda e: e.dma_start(out=VC[kb][:, 0:n], in_=V_d[:, lo:lo + n, :]),
                      reads=[B_V[b] for b in ch], writes=[B_VC[kb]], key="VC%d" % kb)

            kb0 = nchunk % 2
            load_chunk(chunks[0], kb0)
            seq = [(ci, b) for ci, ch in enumerate(chunks) for b in ch]
            for X in range(2):
                z_step(X, kb0, seq[0][1] - chunks[0][-1], True)
            for si, (ci, b) in enumerate(seq):
                kb = (nchunk + ci) % 2
                bi = b - chunks[ci][-1]
                first = (si == 0)
                last = (si == len(seq) - 1)
                if bi == len(chunks[ci]) - 1 and ci + 1 < len(chunks):
                    load_chunk(chunks[ci + 1], (nchunk + ci + 1) % 2)
                for X in range(2):
                    S.add("act", lambda e, X=X: e.activation(eb[X][:], psf(2 * X, 1024), AF.Exp),
                          reads=ZB[X], writes=[B_eb[X]])
                if not first:
                    for X in range(2):
                        def fradd(e, X=X):
                            ins = None
                            for k2 in range(2):
                                ins = e.matmul(psf(2 * X + k2, 512), sel[:, X, :], Rq[X][:, k2 * 512:(k2 + 1) * 512],
                                               start=False, stop=False, skip_group_check=True)
                            return ins
                        S.add("pe", fradd, reads=[B_const, B_Qr[X]], writes=ZB[X])
                for X in range(2):
                    S.add("act", lambda e, X=X: e.activation(spb[X][:], eb[X][:], AF.Ln, bias=1.0),
                          reads=[B_eb[X]], writes=[B_spb[X]])
                for X in range(2):
                    def ftri(e, X=X):
                        ins = None
                        for k2 in range(2):
                            ins = e.matmul(psf(2 * X + k2, 512), tri[:], spb[X][:, k2 * 512:(k2 + 1) * 512],
                                           start=False, stop=True, skip_group_check=True)
                        return ins
                    S.add("pe", ftri, reads=[B_spb[X], B_const], writes=ZB[X])
                    if not last:
                        def ferow(e, X=X, first=first):
                            ins = None
                            for k2 in range(2):
                                mx = 65 if X == 0 else 33
                                ins = e.matmul(psf(4 + k2, 512)[0:mx, :], erow[:, 128 * X:128 * X + mx],
                                               spb[X][:, k2 * 512:(k2 + 1) * 512],
                                               start=(first and X == 0), stop=False, skip_group_check=True)
                            return ins
                        S.add("pe", ferow, reads=[B_spb[X], B_const], writes=[B_R[X]])
                        a = AUG[X]
                        S.add("dve", lambda e, X=X, a=a: e.tensor_copy(
                            Rq[X][a:a + 1, :], psf(4, 1024)[a:a + 1, :]),
                            reads=[B_R[X]], writes=[B_Qr[X]])
                for X in range(2):
                    S.add("act", lambda e, X=X: e.activation(Ab[X][:], psf(2 * X, 1024), AF.Exp),
                          reads=ZB[X], writes=[B_Ab[X]])
                for X in range(2):
                    if not last:
                        ci2, b2 = seq[si + 1]
                        z_step(X, (nchunk + ci2) % 2, b2 - chunks[ci2][-1], False)

                    def fav(e, X=X, kb=kb, bi=bi, first=first, last=last):
                        ins = None
                        for h in range(8):
                            hd = 2 * h + X
                            ins = e.matmul(psf(6 + X, 512)[:, h * 64:(h + 1) * 64], Ab[X][:, h * 128:(h + 1) * 128],
                                           VC[kb][:, bi, hd * 64:(hd + 1) * 64],
                                           start=(first and h == 0), stop=last, skip_group_check=True)
                        return ins
                    S.add("pe", fav, reads=[B_Ab[X], B_VC[kb]], writes=[OB[X]])
            nchunk += len(chunks)
            for X in range(2):
                S.add("dve", lambda e, X=X: e.tensor_copy(osb[:, :, X, :], psf(6 + X, 512).rearrange("p (h d) -> p h d", h=8)),
                      reads=[OB[X]], pwrites=[B_osb])
            osb2 = osb[:].rearrange("p h x d -> p (h x d)")
            S.add("sp", lambda e, j=j: e.dma_start(out=sgl[:], in_=SG_d[:, j, :]), reads=[B_SG[j]], writes=[B_sgl], key="sgl")
            S.add("sp", lambda e, j=j: e.dma_start(out=og[:, 1024:2048], in_=OGW_d[:, j, :]), reads=[B_OGW[j]],
                  pwrites=[B_og], key="og")
            S.add("sp", lambda e, j=j: e.dma_start(out=xo[:], in_=xs[(8 * j + 7) * 128:(8 * j + 8) * 128, :]),
                  writes=[B_xo], key="xo")
            S.add("act", lambda e: e.activation(jk[:, 0:1024], osb2, AF.Square, accum_out=st4[:, 0:1]),
                  reads=[B_osb], writes=[B_jk, B_st4])
            rstd_chain(st4[:, 0:1], st4[:, 1:2], st4[:, 2:3], 1.0 / 1024, B_st4)
            S.add("dve", lambda e: e.scalar_tensor_tensor(og[:, 0:1024], osb2, st4[:, 2:3], sgl[:], ALU.mult, ALU.mult),
                  reads=[B_osb, B_st4, B_sgl], pwrites=[B_og])

            def ftr4(e):
                ins = None
                for c in range(16):
                    ins = e.transpose(psb(0, 2048)[:, c * 128:(c + 1) * 128], og[:, c * 128:(c + 1) * 128], ident[:])
                return ins
            S.add("pe", ftr4, reads=[B_og, B_const], writes=[B_ps[0], B_ps[1]])
            S.add("act", lambda e: e.activation(ogT[:].rearrange("p c t -> p (c t)"), psb(0, 2048), AF.Copy),
                  reads=[B_ps[0], B_ps[1]], writes=[B_ogT])
            for q4 in range(4):
                def fy(e, q4=q4):
                    ins = None
                    for c in range(16):
                        ins = e.matmul(psf(q4, 512), ogT[:, c, :], Wo[:, c, q4 * 512:(q4 + 1) * 512],
                                       start=(c == 0), stop=(c == 15))
                    return ins
                S.add("pe", fy, reads=[B_ogT, B_Wo], writes=[B_ps[q4]])
            S.add("act", lambda e: e.activation(yb[:], psf(0, 2048), AF.Square, accum_out=st4[:, 3:4]),
                  reads=[B_ps[0], B_ps[1], B_ps[2], B_ps[3]], writes=[B_yb, B_st4])
            rstd_chain(st4[:, 3:4], st4[:, 4:5], st4[:, 5:6], 1.0 / D, B_st4)
            S.add("dve", lambda e: e.scalar_tensor_tensor(yb[:], psf(0, 2048), st4[:, 5:6], npost_s[:], ALU.mult, ALU.mult),
                  reads=[B_ps[0], B_ps[1], B_ps[2], B_ps[3], B_st4, B_Wo], writes=[B_yb])
            S.add("pool", lambda e: e.tensor_tensor(yb[:], yb[:], xo[:], ALU.add), reads=[B_xo], writes=[B_yb])
            S.add("sp", lambda e, j=j: e.dma_start(out=out[j * 128:(j + 1) * 128, :], in_=yb[:]),
                  reads=[B_yb], writes=[Buf("out%d" % j)], key="yb")
        S.add("sp", lambda e: e.nop(), writes=all4 + B_wst4 + [Buf("end")])
        S.emit(nc, st)
    st.close()
    return nc


def silu_from_psum(S, z_ps, B_z, ge, B_ge, gout, B_gout):
    S.add("act", lambda e: e.activation(ge[:], z_ps, AF.Exp, scale=-1.0), reads=B_z, writes=[B_ge])
    S.add("dve", lambda e: e.tensor_scalar(ge[:], ge[:], 1.0, None, ALU.add), reads=[B_ge], writes=[B_ge])
    S.add("dve", lambda e: e.reciprocal(ge[:], ge[:]), reads=[B_ge], writes=[B_ge])
    S.add("dve", lambda e: e.tensor_tensor(gout[:], z_ps, ge[:], ALU.mult), reads=B_z + [B_ge], writes=[B_gout])


def _t5_bucket_table():
    qi = np.arange(128)[:, None]
    ci = np.arange(256)[None, :]
    dist = qi + 128 - ci
    n = np.maximum(dist, 0)
    nf = np.maximum(n, 1).astype(np.float32)
    large = 16 + (np.log(nf / np.float32(16)) / np.float32(np.log(128 / 16)) * np.float32(16)).astype(np.int32)
    large = np.minimum(large, 31)
    bucket = np.where(n < 16, n, large)
    valid = (dist >= 0) & (dist < 128)
    return bucket, valid


def host_inputs(x, w_in, w_out, norm_pre, norm_post, gn_sb, gn_sw, sinks, rel_bias, NS):
    bf = ml_dtypes.bfloat16
    NB = 8 * NS
    x2 = np.ascontiguousarray(x.reshape(-1, D)).astype(np.float32, copy=False)
    ntile = x2.shape[0] // 128
    assert ntile == NB
    bucket, valid = _t5_bucket_table()
    bias = rel_bias.astype(np.float32)[bucket]
    hord = [4 * g + p for g in range(4) for p in (0, 2, 1, 3)]
    biasT = np.ascontiguousarray(bias[:, :, hord].transpose(0, 2, 1)).reshape(128, 16 * 256)
    maskc = np.where(valid, 0.0, NEG).astype(np.float32)
    ident = np.eye(128, dtype=np.float32).astype(bf)
    jj = np.arange(128)[:, None]
    ss = np.arange(128)[None, :]
    tri = np.where(jj >= ss, -1.0, 0.0).astype(np.float32).astype(bf)
    erow = np.zeros((128, 256), np.float32)
    erow[:, 64] = -1.0
    erow[:, 128 + 32] = -1.0
    erow = erow.astype(bf)
    dm = np.where(jj < ss, 0.0, NEG).astype(np.float32)
    dmask = np.tile(dm, (1, 8)).astype(bf)
    common = dict(
        w_in=np.ascontiguousarray(w_in.reshape(D, DIN)), w_out=np.ascontiguousarray(w_out.reshape(D, D)),
        npre=np.ascontiguousarray(norm_pre.reshape(16, 128).T),
        gsbw=np.ascontiguousarray(np.concatenate([gn_sb.reshape(8, 128), gn_sw.reshape(8, 128)], 0).T),
        npost=np.ascontiguousarray(np.broadcast_to(norm_post.reshape(1, D), (128, D))),
        sinkb=np.ascontiguousarray(np.broadcast_to(sinks.reshape(16)[hord].reshape(1, 16), (128, 16))),
        biasT=biasT, maskc=maskc, ident=ident, tri=tri, erow=erow, dmask=dmask)
    in_maps = []
    for c in range(NCORES):
        pad = 7 - c
        xs = np.zeros((NB * 128, D), np.float32)
        n_real = (NB - pad) * 128
        xs[pad * 128:] = x2[:n_real]
        m0 = np.zeros((128, 256), np.float32)
        if c == 0:
            m0[:, :128] = NEG
        d = dict(common)
        d["xs"] = xs
        d["mask0"] = m0
        in_maps.append(d)
    return in_maps


def run(inputs, NS):
    x = np.asarray(inputs["x"])
    in_maps = host_inputs(x, *[np.asarray(inputs[k], dtype=np.float32) for k in
                               ("w_in", "w_out", "norm_pre", "norm_post", "gn_sb", "gn_sw", "sinks", "rel_bias")], NS=NS)
    nc = build_nc(NS)
    res = run_bass_kernel_spmd(nc, in_maps, core_ids=list(range(NCORES)))
    outs = [np.asarray(r["out"]).reshape(NS, 128, D) for r in res.results]
    full = np.stack(outs, axis=1)
    return full.reshape(1, NS * 8 * 128, D).astype(np.float32)


def kernel(x, w_in, w_out, norm_pre, norm_post, gn_sb, gn_sw, sinks, rel_bias):
    return run(dict(x=x, w_in=w_in, w_out=w_out, norm_pre=norm_pre, norm_post=norm_post, gn_sb=gn_sb,
                    gn_sw=gn_sw, sinks=sinks, rel_bias=rel_bias), NS=16)
```

```python
import numpy as np
import ml_dtypes
from contextlib import ExitStack
import concourse.bass as bass
import concourse.mybir as mybir
from concourse.bass_utils import run_bass_kernel_spmd

F32 = mybir.dt.float32
BF16 = mybir.dt.bfloat16
AF = mybir.ActivationFunctionType
ALU = mybir.AluOpType
AX = mybir.AxisListType

D = 2048
DIN = 6656
NCORES = 8
NEG = -30000.0
EPS = 1e-6


class Buf:
    def __init__(self, name):
        self.name = name
        self.writers = []
        self.readers = []
        self.prev = set()
        self.gen_open = False


class Op:
    __slots__ = ("eng", "fn", "deps", "idx", "key", "token", "need")

    def __init__(self, eng, fn, deps, idx, key):
        self.eng, self.fn, self.deps, self.idx, self.key = eng, fn, deps, idx, key
        self.token = None
        self.need = False


class Sched:
    ENGS = ("pe", "act", "dve", "pool", "sp")

    def __init__(self):
        self.ops = []

    def add(self, eng, fn, reads=(), writes=(), pwrites=(), key=None):
        idx = len(self.ops)
        deps = set()
        for b in reads:
            deps.update(b.writers)
        for b in writes:
            prev = set(b.writers) | set(b.readers)
            deps.update(prev)
            b.prev, b.writers, b.readers, b.gen_open = prev, [idx], [], False
        for b in pwrites:
            if b.gen_open and not b.readers:
                deps.update(b.prev)
                b.writers.append(idx)
            else:
                prev = set(b.writers) | set(b.readers)
                deps.update(prev)
                b.prev, b.writers, b.readers, b.gen_open = prev, [idx], [], True
        for b in reads:
            b.readers.append(idx)
        deps.discard(idx)
        self.ops.append(Op(eng, fn, deps, idx, key))
        return idx

    def emit(self, nc, stack):
        ops = self.ops
        for op in ops:
            for d in op.deps:
                dop = ops[d]
                if dop.eng == "pe" and op.eng == "pe" and dop.key is None and op.key is None:
                    continue
                dop.need = True
        sems = {}

        def sem(name):
            if name not in sems:
                sems[name] = stack.enter_context(nc.semaphore("s_" + name))
            return sems[name]

        counts = {}
        for op in ops:
            if op.key is not None:
                k = "d_" + op.key
                counts[k] = counts.get(k, 0) + 16
                op.token = (k, counts[k])
            elif op.need:
                k = "e_" + op.eng
                counts[k] = counts.get(k, 0) + 1
                op.token = (k, counts[k])
        per = {e: [] for e in self.ENGS}
        for op in ops:
            per[op.eng].append(op)
        block = stack.enter_context(nc.Block())

        def body(engname):
            def run(eng):
                waited = {}
                for op in per[engname]:
                    need = {}
                    for d in op.deps:
                        dop = ops[d]
                        if dop.token is None:
                            continue
                        if dop.eng == "pe" and engname == "pe" and dop.key is None and op.key is None:
                            continue
                        k, v = dop.token
                        if need.get(k, 0) < v:
                            need[k] = v
                    for k, v in need.items():
                        if waited.get(k, 0) >= v:
                            continue
                        eng.wait_ge(sem(k), v)
                        waited[k] = v
                    ins = op.fn(eng)
                    if op.token is not None:
                        k, v = op.token
                        ins.then_inc(sem(k), 16 if op.key is not None else 1)
                if engname == "sp":
                    for k, v in counts.items():
                        if waited.get(k, 0) < v:
                            eng.wait_ge(sem(k), v)
            return run

        block.tensor(body("pe"))
        block.scalar(body("act"))
        block.vector(body("dve"))
        block.gpsimd(body("pool"))
        block.sync(body("sp"))


def build_nc(NS):
    NB = 8 * NS
    nc = bass.Bass("TRN2", target_bir_lowering=False)
    S = Sched()
    st = ExitStack()

    def dram_in(name, shape, dt=F32):
        return nc.dram_tensor(name, list(shape), dt, kind="ExternalInput").ap()

    xs = dram_in("xs", [NB * 128, D])
    w_in = dram_in("w_in", [D, DIN])
    w_out = dram_in("w_out", [D, D])
    npre = dram_in("npre", [128, 16])
    gsbw = dram_in("gsbw", [128, 16])
    npost = dram_in("npost", [128, D])
    sinkb = dram_in("sinkb", [128, 16])
    biasT = dram_in("biasT", [128, 16 * 256])
    maskc = dram_in("maskc", [128, 256])
    mask0 = dram_in("mask0", [128, 256])
    ident_d = dram_in("ident", [128, 128], BF16)
    tri_d = dram_in("tri", [128, 128], BF16)
    erow_d = dram_in("erow", [128, 256], BF16)
    dmask_d = dram_in("dmask", [128, 1024], BF16)
    out = nc.dram_tensor("out", [NS * 128, D], F32, kind="ExternalOutput").ap()

    KT_d = nc.dram_tensor("KT_d", [128, NB, 8, 128], BF16).ap()
    V_d = nc.dram_tensor("V_d", [128, NB, 1024], BF16).ap()
    QT_d = nc.dram_tensor("QT_d", [128, NS, 8, 128], BF16).ap()
    SG_d = nc.dram_tensor("SG_d", [128, NS, 1024], BF16).ap()
    OGW_d = nc.dram_tensor("OGW_d", [128, NS, 1024], BF16).ap()
    B_KT = [Buf("KT_d%d" % i) for i in range(NB)]
    B_V = [Buf("V_d%d" % i) for i in range(NB)]
    B_QT = [Buf("QT_d%d" % j) for j in range(NS)]
    B_SG = [Buf("SG_d%d" % j) for j in range(NS)]
    B_OGW = [Buf("OGW_d%d" % j) for j in range(NS)]

    ARENA_BYTES = 204 * 1024
    arena = st.enter_context(nc.sbuf_tensor("arena", [128, ARENA_BYTES // 2], BF16))
    apos = [0]

    class _T:
        def __init__(self, ap):
            self.ap = ap

        def __getitem__(self, key):
            return self.ap[key]

    def sb(name, shape, dt):
        nfree = 1
        for s_ in shape[1:]:
            nfree *= s_
        nbytes = nfree * (4 if dt == F32 else 2)
        nbytes = (nbytes + 63) // 64 * 64
        off = apos[0]
        apos[0] += nbytes
        assert apos[0] <= ARENA_BYTES, (name, apos[0])
        v = arena[:, off // 2:(off + nbytes) // 2]
        if dt == F32:
            v = v.bitcast(F32)
        v = v[:, 0:nfree]
        if len(shape) == 3:
            v = v.rearrange("p (a b) -> p a b", a=shape[1])
        elif len(shape) == 4:
            v = v.rearrange("p (a b c) -> p a b c", a=shape[1], b=shape[2])
        return _T(v)

    ps = st.enter_context(nc.psum_tensor("ps", [128, 8 * 512], F32))
    B_ps = [Buf("bank%d" % k) for k in range(8)]

    def psf(b0, ncols):
        return ps[:, b0 * 512: b0 * 512 + ncols]

    def psb(b0, ncols):
        return ps[:, b0 * 512: b0 * 512 + ncols // 2].bitcast(BF16)

    ident = sb("ident", [128, 128], BF16)
    tri = sb("tri", [128, 128], BF16)
    erow = sb("erow", [128, 256], BF16)
    dmask = sb("dmask", [128, 1024], BF16)
    npre_s = sb("npre_s", [128, 16], F32)
    gsbw_s = sb("gsbw_s", [128, 16], F32)
    B_const = Buf("const")

    def ld_const(dst, src):
        S.add("sp", lambda e: e.dma_start(out=dst, in_=src), pwrites=[B_const], key="const")
    ld_const(ident[:], ident_d)
    ld_const(tri[:], tri_d)
    ld_const(erow[:], erow_d)
    ld_const(dmask[:], dmask_d)
    ld_const(npre_s[:], npre)
    ld_const(gsbw_s[:], gsbw)

    xt = [sb("xt%d" % k, [128, D], F32) for k in range(2)]
    B_xt = [Buf("xt%d" % k) for k in range(2)]
    xjunk = sb("xjunk", [128, D], BF16)
    B_xjunk = Buf("xjunk")
    xb = [sb("xb%d" % k, [128, D], BF16) for k in range(2)]
    B_xb = [Buf("xb%d" % k) for k in range(2)]
    uT = [sb("uT%d" % k, [128, 16, 128], BF16) for k in range(2)]
    B_uT = [Buf("uT%d" % k) for k in range(2)]
    stat = sb("stat", [128, 8], F32)
    B_stat = Buf("stat")
    wst = [sb("wst%d" % k, [128, 2048], F32) for k in range(2)]
    B_wst = [Buf("wst%d" % k) for k in range(2)]

    def load_weights(W, B_W, colranges, scale_ap, nchunks=16, src=None):
        src = w_in if src is None else src
        n = 0
        for c in range(nchunks):
            off = 0
            for (c0, c1) in colranges:
                w = c1 - c0
                k = n % 2
                n += 1
                S.add("sp", lambda e, k=k, c=c, c0=c0, c1=c1, w=w: e.dma_start(
                    out=wst[k][:, 0:w], in_=src[c * 128:(c + 1) * 128, c0:c1]),
                    writes=[B_wst[k]], key="wst%d" % k)
                S.add("dve", lambda e, k=k, c=c, off=off, w=w: e.tensor_scalar(
                    W[:, c, off:off + w], wst[k][:, 0:w], scale_ap[:, c:c + 1], None, ALU.mult),
                    reads=[B_wst[k], B_const], pwrites=[B_W])
                off += w

    def norm_transpose(i, cnt, tpb=(0, 0)):
        k = cnt % 2
        tb = tpb[k]
        S.add("sp", lambda e: e.dma_start(out=xt[k][:], in_=xs[i * 128:(i + 1) * 128, :]),
              writes=[B_xt[k]], key="xt%d" % k)
        S.add("act", lambda e: e.activation(xjunk[:], xt[k][:], AF.Square, accum_out=stat[:, 0:1]),
              reads=[B_xt[k]], writes=[B_xjunk, B_stat])
        rstd_chain(stat[:, 0:1], stat[:, 1:2], stat[:, 2:3], 1.0 / D, B_stat)
        S.add("dve", lambda e: e.tensor_scalar(xb[k][:], xt[k][:], stat[:, 2:3], None, ALU.mult),
              reads=[B_xt[k], B_stat], writes=[B_xb[k]])

        def tr(e):
            ins = None
            for c in range(16):
                ins = e.transpose(psb(tb, 2048)[:, c * 128:(c + 1) * 128], xb[k][:, c * 128:(c + 1) * 128], ident[:])
            return ins
        S.add("pe", tr, reads=[B_xb[k], B_const], writes=[B_ps[tb], B_ps[tb + 1]])
        S.add("act", lambda e: e.activation(uT[k][:].rearrange("p c t -> p (c t)"), psb(tb, 2048), AF.Copy),
              reads=[B_ps[tb], B_ps[tb + 1]], writes=[B_uT[k]])
        return k

    def rstd_chain(ss, tmp, rstd, inv_n, B):
        S.add("dve", lambda e: e.tensor_scalar(tmp, ss, inv_n, EPS, ALU.mult, ALU.add), reads=[B], writes=[B])
        S.add("act", lambda e: e.activation(tmp, tmp, AF.Ln), reads=[B], writes=[B])
        S.add("act", lambda e: e.activation(rstd, tmp, AF.Exp, scale=-0.5), reads=[B], writes=[B])

    def mm_feat(bank_ap, W, col0, n_m, rhs_of_c, start_first=True):
        def f(e):
            ins = None
            for c in range(16):
                ins = e.matmul(bank_ap, W[:, c, col0:col0 + n_m], rhs_of_c(c),
                               start=(c == 0 and start_first), stop=(c == 15), skip_group_check=True)
            return ins
        return f

    mark = apos[0]
    if True:
        Wkv = sb("Wkv", [128, 16, 2048], BF16)
        B_Wkv = Buf("Wkv")
        kst = [sb("kst%d" % k, [128, 8, 128], BF16) for k in range(2)]
        vst = [sb("vst%d" % k, [128, 1024], BF16) for k in range(2)]
        B_kst = [Buf("kst%d" % k) for k in range(2)]
        B_vst = [Buf("vst%d" % k) for k in range(2)]
        load_weights(Wkv, B_Wkv, [(1024, 3072)], npre_s)
        Wq = sb("Wq", [128, 16, 2048], BF16)
        B_Wq = Buf("Wq")
        qst = sb("qst", [128, 8, 128], BF16)
        B_qst = Buf("qst")
        ge = sb("ge", [128, 1024], F32)
        B_ge = Buf("ge")
        gst = sb("gst", [128, 1024], BF16)
        B_gst = Buf("gst")
        load_weights(Wq, B_Wq, [(0, 1024), (3072, 4096)], npre_s)
        norm_transpose(0, 0, (0, 6))
        for i in range(NB):
            k = i % 2
            if i + 1 < NB:
                norm_transpose(i + 1, i + 1, (0, 6))
            for pr in range(8):
                bk = 2 + pr // 4
                S.add("pe", mm_feat(psf(2, 1024)[:, pr * 128:(pr + 1) * 128], Wkv, pr * 128, 128,
                                    lambda c, k=k: uT[k][:, c, :], start_first=(pr % 4 == 0)),
                      reads=[B_Wkv, B_uT[k]], pwrites=[B_ps[bk]])
            S.add("act", lambda e, k=k: e.activation(kst[k][:].rearrange("p a t -> p (a t)"), psf(2, 1024), AF.Copy),
                  reads=[B_ps[2], B_ps[3]], writes=[B_kst[k]])
            S.add("pool", lambda e, k=k, i=i: e.dma_start(out=KT_d[:, i, :, :], in_=kst[k][:]),
                  reads=[B_kst[k]], writes=[B_KT[i]], key="kst%d" % k)
            for hf in range(2):
                def fv(e, hf=hf, k=k):
                    ins = None
                    for c in range(16):
                        ins = e.matmul(psf(4 + hf, 512), uT[k][:, c, :], Wkv[:, c, 1024 + hf * 512:1024 + (hf + 1) * 512],
                                       start=(c == 0), stop=(c == 15))
                    return ins
                S.add("pe", fv, reads=[B_Wkv, B_uT[k]], writes=[B_ps[4 + hf]])
            S.add("dve", lambda e, k=k: e.tensor_copy(vst[k][:], psf(4, 1024)),
                  reads=[B_ps[4], B_ps[5]], writes=[B_vst[k]])
            S.add("pool", lambda e, k=k, i=i: e.dma_start(out=V_d[:, i, :], in_=vst[k][:]),
                  reads=[B_vst[k]], writes=[B_V[i]], key="vst%d" % k)
            if i % 8 == 7:
                j = i // 8
                for pr in range(8):
                    bk = 2 + pr // 4
                    S.add("pe", mm_feat(psf(2, 1024)[:, pr * 128:(pr + 1) * 128], Wq, pr * 128, 128,
                                        lambda c, k=k: uT[k][:, c, :], start_first=(pr % 4 == 0)),
                          reads=[B_Wq, B_uT[k]], pwrites=[B_ps[bk]])
                S.add("act", lambda e: e.activation(qst[:].rearrange("p a t -> p (a t)"), psf(2, 1024), AF.Copy, scale=0.125),
                      reads=[B_ps[2], B_ps[3]], writes=[B_qst])
                S.add("pool", lambda e, j=j: e.dma_start(out=QT_d[:, j, :, :], in_=qst[:]),
                      reads=[B_qst], writes=[B_QT[j]], key="qst")
                for hf in range(2):
                    def fg(e, hf=hf, k=k):
                        ins = None
                        for c in range(16):
                            ins = e.matmul(psf(4 + hf, 512), uT[k][:, c, :], Wq[:, c, 1024 + hf * 512:1024 + (hf + 1) * 512],
                                           start=(c == 0), stop=(c == 15))
                        return ins
                    S.add("pe", fg, reads=[B_Wq, B_uT[k]], writes=[B_ps[4 + hf]])
                silu_from_psum(S, psf(4, 1024), [B_ps[4], B_ps[5]], ge, B_ge, gst, B_gst)
                S.add("pool", lambda e, j=j: e.dma_start(out=SG_d[:, j, :], in_=gst[:]),
                      reads=[B_gst], writes=[B_SG[j]], key="gst")
        p1_tail = [B_Wkv] + B_kst + B_vst + [B_Wq, B_qst, B_ge, B_gst]
    B_region = Buf("region")
    S.add("sp", lambda e: e.nop(), writes=p1_tail + [B_region])

    apos[0] = mark
    if True:
        W3 = sb("W3", [128, 16, 2560], BF16)
        B_W3 = Buf("W3")
        bias_s = sb("bias_s", [128, 16, 256], F32)
        mtmp = sb("mtmp", [128, 2, 256], F32)
        sink_s = sb("sink_s", [128, 16], F32)
        B_bias = Buf("bias")
        Ksw = sb("Ksw", [128, 4, 256], BF16)
        Vsw = sb("Vsw", [128, 2, 256], BF16)
        Qsw = sb("Qsw", [128, 8, 128], BF16)
        lg = sb("lg", [128, 4, 256], F32)
        pexp = sb("pexp", [128, 4, 256], BF16)
        pT = sb("pT", [128, 4, 2, 128], BF16)
        sm = sb("sm", [128, 32], F32)
        osw = sb("osw", [128, 1024], F32)
        ge3 = sb("ge3", [128, 1024], F32)
        sg3 = sb("sg3", [128, 1024], BF16)
        og3 = sb("og3", [128, 1024], BF16)
        B_Ksw, B_Vsw, B_Qsw, B_lg, B_pexp, B_pT, B_sm, B_osw, B_ge3, B_sg3, B_og3 = [
            Buf(n) for n in ("Ksw", "Vsw", "Qsw", "lg", "pexp", "pT", "sm", "osw", "ge3", "sg3", "og3")]
        S.add("sp", lambda e: e.nop(), reads=[B_region],
              writes=[B_W3, B_bias, B_Ksw, B_Vsw, B_Qsw, B_lg, B_pexp, B_pT, B_sm, B_osw, B_ge3, B_sg3, B_og3])
        load_weights(W3, B_W3, [(4096, 5120), (5120, 5376), (5376, 5632), (5632, 6656)], npre_s)
        W3kd = sb("W3kd", [128, 16, 4, 128], BF16)
        B_W3kd = Buf("W3kd")

        def fdup(e):
            ins = None
            for c in range(16):
                for half in range(2):
                    ins = e.tensor_copy(W3kd[:, c, :, half * 64:(half + 1) * 64],
                                        W3[:, c, 1024:1280].rearrange("p (g d) -> p g d", g=4))
            return ins
        S.add("pool", fdup, reads=[B_W3, B_region], writes=[B_W3kd])
        S.add("sp", lambda e: e.dma_start(out=bias_s[:].rearrange("p h c -> p (h c)"), in_=biasT), pwrites=[B_bias], key="bias")
        S.add("sp", lambda e: e.dma_start(out=mtmp[:, 0, :], in_=maskc), pwrites=[B_bias], key="bias")
        S.add("sp", lambda e: e.dma_start(out=mtmp[:, 1, :], in_=mask0), pwrites=[B_bias], key="bias")
        S.add("sp", lambda e: e.dma_start(out=sink_s[:], in_=sinkb), pwrites=[B_bias], key="bias")

        def bias_setup(e):
            ins = None
            for h in range(16):
                ins = e.tensor_tensor(bias_s[:, h, :], bias_s[:, h, :], mtmp[:, 0, :], ALU.add)
            return ins
        S.add("pool", bias_setup, reads=[B_bias], writes=[B_bias])

        norm_transpose(6, 0)
        norm_transpose(7, 1)
        for j in range(NS):
            for which, i in ((0, 8 * j + 6), (1, 8 * j + 7)):
                k = which
                for g in range(4):
                    def fk(e, g=g, k=k):
                        ins = None
                        for c in range(16):
                            ins = e.matmul(psf(2, 512)[:, g * 128:(g + 1) * 128], W3kd[:, c, g, :], uT[k][:, c, :],
                                           start=(c == 0 and g == 0), stop=(c == 15), skip_group_check=True)
                        return ins
                    S.add("pe", fk, reads=[B_W3kd, B_uT[k]], pwrites=[B_ps[2]])
                S.add("act", lambda e, which=which: e.activation(
                    Ksw[:, :, which * 128:(which + 1) * 128], psf(2, 512).rearrange("p (g t) -> p g t", g=4), AF.Copy),
                    reads=[B_ps[2]], pwrites=[B_Ksw])

                def fv3(e, k=k):
                    ins = None
                    for c in range(16):
                        ins = e.matmul(psf(3, 256), uT[k][:, c, :], W3[:, c, 1280:1536], start=(c == 0), stop=(c == 15))
                    return ins
                S.add("pe", fv3, reads=[B_W3, B_uT[k]], writes=[B_ps[3]])
                S.add("dve", lambda e, which=which: e.tensor_copy(Vsw[:, which, :], psf(3, 256)),
                      reads=[B_ps[3]], pwrites=[B_Vsw])
            for pr in range(8):
                bk = 4 + pr // 4
                S.add("pe", mm_feat(psf(4, 1024)[:, pr * 128:(pr + 1) * 128], W3, pr * 128, 128,
                                    lambda c, k=k: uT[k][:, c, :], start_first=(pr % 4 == 0)),
                      reads=[B_W3, B_uT[k]], pwrites=[B_ps[bk]])
            S.add("act", lambda e: e.activation(Qsw[:].rearrange("p a t -> p (a t)"), psf(4, 1024), AF.Copy, scale=0.125),
                  reads=[B_ps[4], B_ps[5]], writes=[B_Qsw])
            for hf in range(2):
                def fg3(e, hf=hf, k=k):
                    ins = None
                    for c in range(16):
                        ins = e.matmul(psf(6 + hf, 512), uT[k][:, c, :], W3[:, c, 1536 + hf * 512:1536 + (hf + 1) * 512],
                                       start=(c == 0), stop=(c == 15))
                    return ins
                S.add("pe", fg3, reads=[B_W3, B_uT[k]], writes=[B_ps[6 + hf]])
            silu_from_psum(S, psf(6, 1024), [B_ps[6], B_ps[7]], ge3, B_ge3, sg3, B_sg3)
            if j + 1 < NS:
                norm_transpose(8 * (j + 1) + 6, 0)
                norm_transpose(8 * (j + 1) + 7, 1)
            btab = bias_s
            PERM = (0, 2, 1, 3)
            for g in range(4):
                def flog(e, g=g):
                    ins = None
                    for s_ in range(4):
                        h = 4 * g + PERM[s_]
                        pb = (h % 2) * 64
                        ins = e.matmul(psf(2, 1024)[:, s_ * 256:(s_ + 1) * 256],
                                       Qsw[pb:pb + 64, h // 2, :], Ksw[pb:pb + 64, g, :],
                                       start=(s_ % 2 == 0), stop=True, skip_group_check=True)
                    return ins
                S.add("pe", flog, reads=[B_Qsw, B_Ksw], writes=[B_ps[2], B_ps[3]])
                S.add("dve", lambda e, g=g, btab=btab: e.tensor_tensor(
                    lg[:], psf(2, 1024).rearrange("p (h c) -> p h c", h=4), btab[:, 4 * g:4 * g + 4, :], ALU.add),
                    reads=[B_ps[2], B_ps[3], B_bias], writes=[B_lg])
                if j == 0:
                    def fm0(e):
                        ins = None
                        for s_ in range(4):
                            ins = e.tensor_tensor(lg[:, s_, :], lg[:, s_, :], mtmp[:, 1, :], ALU.add)
                        return ins
                    S.add("dve", fm0, reads=[B_bias], writes=[B_lg])
                S.add("dve", lambda e: e.tensor_reduce(sm[:, 0:4], lg[:], AX.X, ALU.max), reads=[B_lg], writes=[B_sm])
                S.add("dve", lambda e, g=g: e.tensor_tensor(sm[:, 0:4], sm[:, 0:4], sink_s[:, 4 * g:4 * g + 4], ALU.max),
                      reads=[B_sm, B_bias], writes=[B_sm])
                S.add("dve", lambda e: e.tensor_scalar(sm[:, 4:8], sm[:, 0:4], -1.0, None, ALU.mult),
                      reads=[B_sm], writes=[B_sm])
                S.add("dve", lambda e, g=g: e.tensor_tensor(sm[:, 12:16], sink_s[:, 4 * g:4 * g + 4], sm[:, 0:4], ALU.subtract),
                      reads=[B_sm, B_bias], writes=[B_sm])

                def fexp(e):
                    ins = None
                    for hh in range(4):
                        ins = e.activation(pexp[:, hh, :], lg[:, hh, :], AF.Exp, bias=sm[:, 4 + hh:5 + hh],
                                           accum_out=sm[:, 8 + hh:9 + hh])
                    ins = e.activation(sm[:, 12:16], sm[:, 12:16], AF.Exp)
                    return ins
                S.add("act", fexp, reads=[B_lg, B_sm], writes=[B_pexp, B_sm])
                S.add("dve", lambda e: e.tensor_tensor(sm[:, 8:12], sm[:, 8:12], sm[:, 12:16], ALU.add),
                      reads=[B_sm], writes=[B_sm])
                S.add("dve", lambda e: e.reciprocal(sm[:, 16:20], sm[:, 8:12]), reads=[B_sm], writes=[B_sm])

                def ftr(e):
                    ins = None
                    for hh in range(4):
                        for half in range(2):
                            ins = e.transpose(psb(0, 1024)[:, (hh * 2 + half) * 128:(hh * 2 + half + 1) * 128],
                                              pexp[:, hh, half * 128:(half + 1) * 128], ident[:])
                    return ins
                S.add("pe", ftr, reads=[B_pexp, B_const], writes=[B_ps[0]])
                S.add("act", lambda e: e.activation(pT[:].rearrange("p h a t -> p (h a t)"), psb(0, 1024), AF.Copy),
                      reads=[B_ps[0]], writes=[B_pT])

                def fpv(e, g=g):
                    ins = None
                    for hh in range(4):
                        for half in range(2):
                            ins = e.matmul(psf(1, 256)[:, hh * 64:(hh + 1) * 64], pT[:, hh, half, :],
                                           Vsw[:, half, g * 64:(g + 1) * 64],
                                           start=(hh == 0 and half == 0), stop=(half == 1), skip_group_check=True)
                    return ins
                S.add("pe", fpv, reads=[B_pT, B_Vsw], writes=[B_ps[1]])

                def fnorm(e, g=g):
                    ins = None
                    for s_ in range(4):
                        h = 4 * g + PERM[s_]
                        ins = e.tensor_scalar(osw[:, h * 64:(h + 1) * 64], psf(1, 256)[:, s_ * 64:(s_ + 1) * 64],
                                              sm[:, 16 + s_:17 + s_], None, ALU.mult)
                    return ins
                S.add("dve", fnorm, reads=[B_ps[1], B_sm], pwrites=[B_osw])
            S.add("act", lambda e: e.activation(ge3[:], osw[:], AF.Square, accum_out=sm[:, 20:21]),
                  reads=[B_osw], writes=[B_ge3, B_sm])
            rstd_chain(sm[:, 20:21], sm[:, 21:22], sm[:, 22:23], 1.0 / 1024, B_sm)
            S.add("dve", lambda e: e.scalar_tensor_tensor(og3[:], osw[:], sm[:, 22:23], sg3[:], ALU.mult, ALU.mult),
                  reads=[B_osw, B_sm, B_sg3], writes=[B_og3])
            S.add("pool", lambda e, j=j: e.dma_start(out=OGW_d[:, j, :], in_=og3[:]),
                  reads=[B_og3], writes=[B_OGW[j]], key="og3")
        p3_tail = [B_W3, B_W3kd, B_bias, B_Ksw, B_Vsw, B_Qsw, B_lg, B_pexp, B_pT, B_sm, B_osw, B_ge3, B_sg3, B_og3]
    S.add("sp", lambda e: e.nop(), writes=p3_tail + B_xt + [B_xjunk, B_stat] + B_xb + B_uT + B_wst + [B_region])

    CH = 2
    apos[0] = mark
    if True:
        sb4 = sb
        Wo = sb4("Wo", [128, 16, 2048], BF16)
        B_Wo = Buf("Wo")
        npost_s = sb4("npost_s", [128, D], F32)
        KA = [sb4("KA%d" % k, [128, CH, 8, 128], BF16) for k in range(2)]
        KB = [sb4("KB%d" % k, [128, CH, 8, 128], BF16) for k in range(2)]
        VC = [sb4("VC%d" % k, [128, CH, 1024], BF16) for k in range(2)]
        B_KA = [Buf("KA%d" % k) for k in range(2)]
        B_KB = [Buf("KB%d" % k) for k in range(2)]
        B_VC = [Buf("VC%d" % k) for k in range(2)]
        QA = sb4("QA", [128, 8, 128], BF16)
        QB = sb4("QB", [128, 8, 128], BF16)
        B_QAd, B_QAr, B_QBd, B_QBr = Buf("QAd"), Buf("QAr"), Buf("QBd"), Buf("QBr")
        eb = [sb4("eb%d" % x, [128, 1024], F32) for x in range(2)]
        spb = [sb4("spb%d" % x, [128, 1024], BF16) for x in range(2)]
        Ab = [sb4("Ab%d" % x, [128, 1024], BF16) for x in range(2)]
        B_eb = [Buf("eb%d" % x) for x in range(2)]
        B_spb = [Buf("spb%d" % x) for x in range(2)]
        B_Ab = [Buf("Ab%d" % x) for x in range(2)]
        osb = sb4("osb", [128, 8, 2, 64], F32)
        B_osb = Buf("osb")
        og = sb4("og", [128, 2048], BF16)
        B_og = Buf("og")
        sgl = sb4("sgl", [128, 1024], BF16)
        B_sgl = Buf("sgl")
        ogT = uT[0]
        B_ogT = Buf("ogT")
        xo = xt[0]
        B_xo = Buf("xo")
        yb = xt[1]
        B_yb = Buf("yb")
        jk = xjunk
        B_jk = Buf("jk")
        st4 = sb4("st4", [128, 8], F32)
        B_st4 = Buf("st4")
        all4 = [B_Wo] + B_KA + B_KB + B_VC + [B_QAd, B_QAr, B_QBd, B_QBr] + B_eb + B_spb + B_Ab + \
            [B_osb, B_og, B_sgl, B_ogT, B_xo, B_yb, B_jk, B_st4]
        S.add("sp", lambda e: e.nop(), reads=[B_region], writes=all4)
        wst4 = wst
        B_wst4 = [Buf("wst4_%d" % k) for k in range(2)]
        for c in range(16):
            k = c % 2
            S.add("sp", lambda e, k=k, c=c: e.dma_start(out=wst4[k][:], in_=w_out[c * 128:(c + 1) * 128, :]),
                  writes=[B_wst4[k]], key="wst4_%d" % k)
            S.add("dve", lambda e, k=k, c=c: e.tensor_scalar(Wo[:, c, :], wst4[k][:], gsbw_s[:, c:c + 1], None, ALU.mult),
                  reads=[B_wst4[k], B_const], pwrites=[B_Wo])
        S.add("sp", lambda e: e.dma_start(out=npost_s[:], in_=npost), pwrites=[B_Wo], key="npost")

        Rq = [sb4("Rq%d" % x, [128, 1024], BF16) for x in range(2)]
        sel = sb4("sel", [128, 2, 128], BF16)

        def kconst(e):
            ins = None
            for k in range(2):
                e.memset(KA[k][64:128], 0.0)
                e.memset(KB[k][0:64], 0.0)
            e.memset(QA[64:128], 0.0)
            e.memset(QB[0:64], 0.0)
            for x in range(2):
                e.memset(Rq[x][:], 0.0)
            e.memset(sel[:], 0.0)
            e.memset(sel[64:65, 0, :], 1.0)
            ins = e.memset(sel[32:33, 1, :], 1.0)
            return ins
        S.add("pool", kconst, writes=B_KA + B_KB + [B_QAr, B_QBr, B_QAd, B_QBd])

        ZB = [[B_ps[0], B_ps[1]], [B_ps[2], B_ps[3]]]
        RB = [B_ps[4], B_ps[5]]
        B_R = [Buf("R_A"), Buf("R_B")]
        OB = [B_ps[6], B_ps[7]]
        Qt = [QA, QB]
        Kt = [KA, KB]
        KR = [65, 128]
        AUG = [64, 32]
        B_Qd = [B_QAd, B_QBd]
        B_Qr = [B_QAr, B_QBr]
        B_Kt = [B_KA, B_KB]
        nchunk = 0

        def z_step(X, kb, bi, first):

            def f(e):
                ins = None
                for h in range(8):
                    ins = e.matmul(psf(2 * X, 1024)[:, h * 128:(h + 1) * 128], Kt[X][kb][:, bi, h, :],
                                   Qt[X][:, h, :],
                                   start=(h % 4 == 0), stop=False, skip_group_check=True)
                if first:
                    for k2 in range(2):
                        ins = e.matmul(psf(2 * X + k2, 512), ident[:], dmask[:, k2 * 512:(k2 + 1) * 512],
                                       start=False, stop=False, skip_group_check=True)
                return ins
            S.add("pe", f, reads=[B_Kt[X][kb], B_Qd[X], B_const], writes=ZB[X])

        for j in range(NS):
            nblk = 8 * j + 8
            S.add("sp", lambda e, j=j: e.dma_start(out=QA[0:64], in_=QT_d[0:64, j, :, :]),
                  reads=[B_QT[j]], writes=[B_QAd], key="QAd")
            S.add("sp", lambda e, j=j: e.dma_start(out=QB[64:128], in_=QT_d[64:128, j, :, :]),
                  reads=[B_QT[j]], writes=[B_QBd], key="QBd")
            blocks = list(range(nblk - 1, -1, -1))
            chunks = [blocks[a:a + CH] for a in range(0, nblk, CH)]

            def load_chunk(ch, kb):
                lo = ch[-1]
                n = len(ch)
                S.add("sp", lambda e: e.dma_start(out=KA[kb][0:64, 0:n], in_=KT_d[0:64, lo:lo + n, :, :]),
                      reads=[B_KT[b] for b in ch], writes=[B_KA[kb]], key="KA%d" % kb)
                S.add("sp", lambda e: e.dma_start(out=KB[kb][64:128, 0:n], in_=KT_d[64:128, lo:lo + n, :, :]),
                      reads=[B_KT[b] for b in ch], writes=[B_KB[kb]], key="KB%d" % kb)
                S.add("sp", lambda e: e.dma_start(out=VC[kb][:, 0:n], in_=V_d[:, lo:lo + n, :]),
                      reads=[B_V[b] for b in ch], writes=[B_VC[kb]], key="VC%d" % kb)

            kb0 = nchunk % 2
            load_chunk(chunks[0], kb0)
            seq = [(ci, b) for ci, ch in enumerate(chunks) for b in ch]
            for X in range(2):
                z_step(X, kb0, seq[0][1] - chunks[0][-1], True)
            for si, (ci, b) in enumerate(seq):
                kb = (nchunk + ci) % 2
                bi = b - chunks[ci][-1]
                first = (si == 0)
                last = (si == len(seq) - 1)
                if bi == len(chunks[ci]) - 1 and ci + 1 < len(chunks):
                    load_chunk(chunks[ci + 1], (nchunk + ci + 1) % 2)
                for X in range(2):
                    S.add("act", lambda e, X=X: e.activation(eb[X][:], psf(2 * X, 1024), AF.Exp),
                          reads=ZB[X], writes=[B_eb[X]])
                if not first:
                    for X in range(2):
                        def fradd(e, X=X):
                            ins = None
                            for k2 in range(2):
                                ins = e.matmul(psf(2 * X + k2, 512), sel[:, X, :], Rq[X][:, k2 * 512:(k2 + 1) * 512],
                                               start=False, stop=False, skip_group_check=True)
                            return ins
                        S.add("pe", fradd, reads=[B_const, B_Qr[X]], writes=ZB[X])
                for X in range(2):
                    S.add("act", lambda e, X=X: e.activation(spb[X][:], eb[X][:], AF.Ln, bias=1.0),
                          reads=[B_eb[X]], writes=[B_spb[X]])
                for X in range(2):
                    def ftri(e, X=X):
                        ins = None
                        for k2 in range(2):
                            ins = e.matmul(psf(2 * X + k2, 512), tri[:], spb[X][:, k2 * 512:(k2 + 1) * 512],
                                           start=False, stop=True, skip_group_check=True)
                        return ins
                    S.add("pe", ftri, reads=[B_spb[X], B_const], writes=ZB[X])
                    if not last:
                        def ferow(e, X=X, first=first):
                            ins = None
                            for k2 in range(2):
                                mx = 65 if X == 0 else 33
                                ins = e.matmul(psf(4 + k2, 512)[0:mx, :], erow[:, 128 * X:128 * X + mx],
                                               spb[X][:, k2 * 512:(k2 + 1) * 512],
                                               start=(first and X == 0), stop=False, skip_group_check=True)
                            return ins
                        S.add("pe", ferow, reads=[B_spb[X], B_const], writes=[B_R[X]])
                        a = AUG[X]
                        S.add("dve", lambda e, X=X, a=a: e.tensor_copy(
                            Rq[X][a:a + 1, :], psf(4, 1024)[a:a + 1, :]),
                            reads=[B_R[X]], writes=[B_Qr[X]])
                for X in range(2):
                    S.add("act", lambda e, X=X: e.activation(Ab[X][:], psf(2 * X, 1024), AF.Exp),
                          reads=ZB[X], writes=[B_Ab[X]])
                for X in range(2):
                    if not last:
                        ci2, b2 = seq[si + 1]
                        z_step(X, (nchunk + ci2) % 2, b2 - chunks[ci2][-1], False)

                    def fav(e, X=X, kb=kb, bi=bi, first=first, last=last):
                        ins = None
                        for h in range(8):
                            hd = 2 * h + X
                            ins = e.matmul(psf(6 + X, 512)[:, h * 64:(h + 1) * 64], Ab[X][:, h * 128:(h + 1) * 128],
                                           VC[kb][:, bi, hd * 64:(hd + 1) * 64],
                                           start=(first and h == 0), stop=last, skip_group_check=True)
                        return ins
                    S.add("pe", fav, reads=[B_Ab[X], B_VC[kb]], writes=[OB[X]])
            nchunk += len(chunks)
            for X in range(2):
                S.add("dve", lambda e, X=X: e.tensor_copy(osb[:, :, X, :], psf(6 + X, 512).rearrange("p (h d) -> p h d", h=8)),
                      reads=[OB[X]], pwrites=[B_osb])
            osb2 = osb[:].rearrange("p h x d -> p (h x d)")
            S.add("sp", lambda e, j=j: e.dma_start(out=sgl[:], in_=SG_d[:, j, :]), reads=[B_SG[j]], writes=[B_sgl], key="sgl")
            S.add("sp", lambda e, j=j: e.dma_start(out=og[:, 1024:2048], in_=OGW_d[:, j, :]), reads=[B_OGW[j]],
                  pwrites=[B_og], key="og")
            S.add("sp", lambda e, j=j: e.dma_start(out=xo[:], in_=xs[(8 * j + 7) * 128:(8 * j + 8) * 128, :]),
                  writes=[B_xo], key="xo")
            S.add("act", lambda e: e.activation(jk[:, 0:1024], osb2, AF.Square, accum_out=st4[:, 0:1]),
                  reads=[B_osb], writes=[B_jk, B_st4])
            rstd_chain(st4[:, 0:1], st4[:, 1:2], st4[:, 2:3], 1.0 / 1024, B_st4)
            S.add("dve", lambda e: e.scalar_tensor_tensor(og[:, 0:1024], osb2, st4[:, 2:3], sgl[:], ALU.mult, ALU.mult),
                  reads=[B_osb, B_st4, B_sgl], pwrites=[B_og])

            def ftr4(e):
                ins = None
                for c in range(16):
                    ins = e.transpose(psb(0, 2048)[:, c * 128:(c + 1) * 128], og[:, c * 128:(c + 1) * 128], ident[:])
                return ins
            S.add("pe", ftr4, reads=[B_og, B_const], writes=[B_ps[0], B_ps[1]])
            S.add("act", lambda e: e.activation(ogT[:].rearrange("p c t -> p (c t)"), psb(0, 2048), AF.Copy),
                  reads=[B_ps[0], B_ps[1]], writes=[B_ogT])
            for q4 in range(4):
                def fy(e, q4=q4):
                    ins = None
                    for c in range(16):
                        ins = e.matmul(psf(q4, 512), ogT[:, c, :], Wo[:, c, q4 * 512:(q4 + 1) * 512],
                                       start=(c == 0), stop=(c == 15))
                    return ins
                S.add("pe", fy, reads=[B_ogT, B_Wo], writes=[B_ps[q4]])
            S.add("act", lambda e: e.activation(yb[:], psf(0, 2048), AF.Square, accum_out=st4[:, 3:4]),
                  reads=[B_ps[0], B_ps[1], B_ps[2], B_ps[3]], writes=[B_yb, B_st4])
            rstd_chain(st4[:, 3:4], st4[:, 4:5], st4[:, 5:6], 1.0 / D, B_st4)
            S.add("dve", lambda e: e.scalar_tensor_tensor(yb[:], psf(0, 2048), st4[:, 5:6], npost_s[:], ALU.mult, ALU.mult),
                  reads=[B_ps[0], B_ps[1], B_ps[2], B_ps[3], B_st4, B_Wo], writes=[B_yb])
            S.add("pool", lambda e: e.tensor_tensor(yb[:], yb[:], xo[:], ALU.add), reads=[B_xo], writes=[B_yb])
            S.add("sp", lambda e, j=j: e.dma_start(out=out[j * 128:(j + 1) * 128, :], in_=yb[:]),
                  reads=[B_yb], writes=[Buf("out%d" % j)], key="yb")
        S.add("sp", lambda e: e.nop(), writes=all4 + B_wst4 + [Buf("end")])
        S.emit(nc, st)
    st.close()
    return nc


def silu_from_psum(S, z_ps, B_z, ge, B_ge, gout, B_gout):
    S.add("act", lambda e: e.activation(ge[:], z_ps, AF.Exp, scale=-1.0), reads=B_z, writes=[B_ge])
    S.add("dve", lambda e: e.tensor_scalar(ge[:], ge[:], 1.0, None, ALU.add), reads=[B_ge], writes=[B_ge])
    S.add("dve", lambda e: e.reciprocal(ge[:], ge[:]), reads=[B_ge], writes=[B_ge])
    S.add("dve", lambda e: e.tensor_tensor(gout[:], z_ps, ge[:], ALU.mult), reads=B_z + [B_ge], writes=[B_gout])


def _t5_bucket_table():
    qi = np.arange(128)[:, None]
    ci = np.arange(256)[None, :]
    dist = qi + 128 - ci
    n = np.maximum(dist, 0)
    nf = np.maximum(n, 1).astype(np.float32)
    large = 16 + (np.log(nf / np.float32(16)) / np.float32(np.log(128 / 16)) * np.float32(16)).astype(np.int32)
    large = np.minimum(large, 31)
    bucket = np.where(n < 16, n, large)
    valid = (dist >= 0) & (dist < 128)
    return bucket, valid


def host_inputs(x, w_in, w_out, norm_pre, norm_post, gn_sb, gn_sw, sinks, rel_bias, NS):
    bf = ml_dtypes.bfloat16
    NB = 8 * NS
    x2 = np.ascontiguousarray(x.reshape(-1, D)).astype(np.float32, copy=False)
    ntile = x2.shape[0] // 128
    assert ntile == NB
    bucket, valid = _t5_bucket_table()
    bias = rel_bias.astype(np.float32)[bucket]
    hord = [4 * g + p for g in range(4) for p in (0, 2, 1, 3)]
    biasT = np.ascontiguousarray(bias[:, :, hord].transpose(0, 2, 1)).reshape(128, 16 * 256)
    maskc = np.where(valid, 0.0, NEG).astype(np.float32)
    ident = np.eye(128, dtype=np.float32).astype(bf)
    jj = np.arange(128)[:, None]
    ss = np.arange(128)[None, :]
    tri = np.where(jj >= ss, -1.0, 0.0).astype(np.float32).astype(bf)
    erow = np.zeros((128, 256), np.float32)
    erow[:, 64] = -1.0
    erow[:, 128 + 32] = -1.0
    erow = erow.astype(bf)
    dm = np.where(jj < ss, 0.0, NEG).astype(np.float32)
    dmask = np.tile(dm, (1, 8)).astype(bf)
    common = dict(
        w_in=np.ascontiguousarray(w_in.reshape(D, DIN)), w_out=np.ascontiguousarray(w_out.reshape(D, D)),
        npre=np.ascontiguousarray(norm_pre.reshape(16, 128).T),
        gsbw=np.ascontiguousarray(np.concatenate([gn_sb.reshape(8, 128), gn_sw.reshape(8, 128)], 0).T),
        npost=np.ascontiguousarray(np.broadcast_to(norm_post.reshape(1, D), (128, D))),
        sinkb=np.ascontiguousarray(np.broadcast_to(sinks.reshape(16)[hord].reshape(1, 16), (128, 16))),
        biasT=biasT, maskc=maskc, ident=ident, tri=tri, erow=erow, dmask=dmask)
    in_maps = []
    for c in range(NCORES):
        pad = 7 - c
        xs = np.zeros((NB * 128, D), np.float32)
        n_real = (NB - pad) * 128
        xs[pad * 128:] = x2[:n_real]
        m0 = np.zeros((128, 256), np.float32)
        if c == 0:
            m0[:, :128] = NEG
        d = dict(common)
        d["xs"] = xs
        d["mask0"] = m0
        in_maps.append(d)
    return in_maps


def run(inputs, NS):
    x = np.asarray(inputs["x"])
    in_maps = host_inputs(x, *[np.asarray(inputs[k], dtype=np.float32) for k in
                               ("w_in", "w_out", "norm_pre", "norm_post", "gn_sb", "gn_sw", "sinks", "rel_bias")], NS=NS)
    nc = build_nc(NS)
    res = run_bass_kernel_spmd(nc, in_maps, core_ids=list(range(NCORES)))
    outs = [np.asarray(r["out"]).reshape(NS, 128, D) for r in res.results]
    full = np.stack(outs, axis=1)
    return full.reshape(1, NS * 8 * 128, D).astype(np.float32)


def kernel(x, w_in, w_out, norm_pre, norm_post, gn_sb, gn_sw, sinks, rel_bias):
    return run(dict(x=x, w_in=w_in, w_out=w_out, norm_pre=norm_pre, norm_post=norm_post, gn_sb=gn_sb,
                    gn_sw=gn_sw, sinks=sinks, rel_bias=rel_bias), NS=16)
```

```python
import numpy as np
import ml_dtypes
from contextlib import ExitStack
import concourse.bass as bass
import concourse.mybir as mybir
from concourse.bass_utils import run_bass_kernel_spmd

F32 = mybir.dt.float32
BF16 = mybir.dt.bfloat16
AF = mybir.ActivationFunctionType
ALU = mybir.AluOpType
AX = mybir.AxisListType

D = 2048
DIN = 6656
NCORES = 8
NEG = -30000.0
EPS = 1e-6


class Buf:
    def __init__(self, name):
        self.name = name
        self.writers = []
        self.readers = []
        self.prev = set()
        self.gen_open = False


class Op:
    __slots__ = ("eng", "fn", "deps", "idx", "key", "token", "need")

    def __init__(self, eng, fn, deps, idx, key):
        self.eng, self.fn, self.deps, self.idx, self.key = eng, fn, deps, idx, key
        self.token = None
        self.need = False


class Sched:
    ENGS = ("pe", "act", "dve", "pool", "sp")

    def __init__(self):
        self.ops = []

    def add(self, eng, fn, reads=(), writes=(), pwrites=(), key=None):
        idx = len(self.ops)
        deps = set()
        for b in reads:
            deps.update(b.writers)
        for b in writes:
            prev = set(b.writers) | set(b.readers)
            deps.update(prev)
            b.prev, b.writers, b.readers, b.gen_open = prev, [idx], [], False
        for b in pwrites:
            if b.gen_open and not b.readers:
                deps.update(b.prev)
                b.writers.append(idx)
            else:
                prev = set(b.writers) | set(b.readers)
                deps.update(prev)
                b.prev, b.writers, b.readers, b.gen_open = prev, [idx], [], True
        for b in reads:
            b.readers.append(idx)
        deps.discard(idx)
        self.ops.append(Op(eng, fn, deps, idx, key))
        return idx

    def emit(self, nc, stack):
        ops = self.ops
        for op in ops:
            for d in op.deps:
                dop = ops[d]
                if dop.eng == "pe" and op.eng == "pe" and dop.key is None and op.key is None:
                    continue
                dop.need = True
        sems = {}

        def sem(name):
            if name not in sems:
                sems[name] = stack.enter_context(nc.semaphore("s_" + name))
            return sems[name]

        counts = {}
        for op in ops:
            if op.key is not None:
                k = "d_" + op.key
                counts[k] = counts.get(k, 0) + 16
                op.token = (k, counts[k])
            elif op.need:
                k = "e_" + op.eng
                counts[k] = counts.get(k, 0) + 1
                op.token = (k, counts[k])
        per = {e: [] for e in self.ENGS}
        for op in ops:
            per[op.eng].append(op)
        block = stack.enter_context(nc.Block())

        def body(engname):
            def run(eng):
                waited = {}
                for op in per[engname]:
                    need = {}
                    for d in op.deps:
                        dop = ops[d]
                        if dop.token is None:
                            continue
                        if dop.eng == "pe" and engname == "pe" and dop.key is None and op.key is None:
                            continue
                        k, v = dop.token
                        if need.get(k, 0) < v:
                            need[k] = v
                    for k, v in need.items():
                        if waited.get(k, 0) >= v:
                            continue
                        eng.wait_ge(sem(k), v)
                        waited[k] = v
                    ins = op.fn(eng)
                    if op.token is not None:
                        k, v = op.token
                        ins.then_inc(sem(k), 16 if op.key is not None else 1)
                if engname == "sp":
                    for k, v in counts.items():
                        if waited.get(k, 0) < v:
                            eng.wait_ge(sem(k), v)
            return run

        block.tensor(body("pe"))
        block.scalar(body("act"))
        block.vector(body("dve"))
        block.gpsimd(body("pool"))
        block.sync(body("sp"))


def build_nc(NS):
    NB = 8 * NS
    nc = bass.Bass("TRN2", target_bir_lowering=False)
    S = Sched()
    st = ExitStack()

    def dram_in(name, shape, dt=F32):
        return nc.dram_tensor(name, list(shape), dt, kind="ExternalInput").ap()

    xs = dram_in("xs", [NB * 128, D])
    w_in = dram_in("w_in", [D, DIN])
    w_out = dram_in("w_out", [D, D])
    npre = dram_in("npre", [128, 16])
    gsbw = dram_in("gsbw", [128, 16])
    npost = dram_in("npost", [128, D])
    sinkb = dram_in("sinkb", [128, 16])
    biasT = dram_in("biasT", [128, 16 * 256])
    maskc = dram_in("maskc", [128, 256])
    mask0 = dram_in("mask0", [128, 256])
    ident_d = dram_in("ident", [128, 128], BF16)
    tri_d = dram_in("tri", [128, 128], BF16)
    erow_d = dram_in("erow", [128, 256], BF16)
    dmask_d = dram_in("dmask", [128, 1024], BF16)
    out = nc.dram_tensor("out", [NS * 128, D], F32, kind="ExternalOutput").ap()

    KT_d = nc.dram_tensor("KT_d", [128, NB, 8, 128], BF16).ap()
    V_d = nc.dram_tensor("V_d", [128, NB, 1024], BF16).ap()
    QT_d = nc.dram_tensor("QT_d", [128, NS, 8, 128], BF16).ap()
    SG_d = nc.dram_tensor("SG_d", [128, NS, 1024], BF16).ap()
    OGW_d = nc.dram_tensor("OGW_d", [128, NS, 1024], BF16).ap()
    B_KT = [Buf("KT_d%d" % i) for i in range(NB)]
    B_V = [Buf("V_d%d" % i) for i in range(NB)]
    B_QT = [Buf("QT_d%d" % j) for j in range(NS)]
    B_SG = [Buf("SG_d%d" % j) for j in range(NS)]
    B_OGW = [Buf("OGW_d%d" % j) for j in range(NS)]

    ARENA_BYTES = 204 * 1024
    arena = st.enter_context(nc.sbuf_tensor("arena", [128, ARENA_BYTES // 2], BF16))
    apos = [0]

    class _T:
        def __init__(self, ap):
            self.ap = ap

        def __getitem__(self, key):
            return self.ap[key]

    def sb(name, shape, dt):
        nfree = 1
        for s_ in shape[1:]:
            nfree *= s_
        nbytes = nfree * (4 if dt == F32 else 2)
        nbytes = (nbytes + 63) // 64 * 64
        off = apos[0]
        apos[0] += nbytes
        assert apos[0] <= ARENA_BYTES, (name, apos[0])
        v = arena[:, off // 2:(off + nbytes) // 2]
        if dt == F32:
            v = v.bitcast(F32)
        v = v[:, 0:nfree]
        if len(shape) == 3:
            v = v.rearrange("p (a b) -> p a b", a=shape[1])
        elif len(shape) == 4:
            v = v.rearrange("p (a b c) -> p a b c", a=shape[1], b=shape[2])
        return _T(v)

    ps = st.enter_context(nc.psum_tensor("ps", [128, 8 * 512], F32))
    B_ps = [Buf("bank%d" % k) for k in range(8)]

    def psf(b0, ncols):
        return ps[:, b0 * 512: b0 * 512 + ncols]

    def psb(b0, ncols):
        return ps[:, b0 * 512: b0 * 512 + ncols // 2].bitcast(BF16)

    ident = sb("ident", [128, 128], BF16)
    tri = sb("tri", [128, 128], BF16)
    erow = sb("erow", [128, 256], BF16)
    dmask = sb("dmask", [128, 1024], BF16)
    npre_s = sb("npre_s", [128, 16], F32)
    gsbw_s = sb("gsbw_s", [128, 16], F32)
    B_const = Buf("const")

    def ld_const(dst, src):
        S.add("sp", lambda e: e.dma_start(out=dst, in_=src), pwrites=[B_const], key="const")
    ld_const(ident[:], ident_d)
    ld_const(tri[:], tri_d)
    ld_const(erow[:], erow_d)
    ld_const(dmask[:], dmask_d)
    ld_const(npre_s[:], npre)
    ld_const(gsbw_s[:], gsbw)

    xt = [sb("xt%d" % k, [128, D], F32) for k in range(2)]
    B_xt = [Buf("xt%d" % k) for k in range(2)]
    xjunk = sb("xjunk", [128, D], BF16)
    B_xjunk = Buf("xjunk")
    xb = [sb("xb%d" % k, [128, D], BF16) for k in range(2)]
    B_xb = [Buf("xb%d" % k) for k in range(2)]
    uT = [sb("uT%d" % k, [128, 16, 128], BF16) for k in range(2)]
    B_uT = [Buf("uT%d" % k) for k in range(2)]
    stat = sb("stat", [128, 8], F32)
    B_stat = Buf("stat")
    wst = [sb("wst%d" % k, [128, 2048], F32) for k in range(2)]
    B_wst = [Buf("wst%d" % k) for k in range(2)]

    def load_weights(W, B_W, colranges, scale_ap, nchunks=16, src=None):
        src = w_in if src is None else src
        n = 0
        for c in range(nchunks):
            off = 0
            for (c0, c1) in colranges:
                w = c1 - c0
                k = n % 2
                n += 1
                S.add("sp", lambda e, k=k, c=c, c0=c0, c1=c1, w=w: e.dma_start(
                    out=wst[k][:, 0:w], in_=src[c * 128:(c + 1) * 128, c0:c1]),
                    writes=[B_wst[k]], key="wst%d" % k)
                S.add("dve", lambda e, k=k, c=c, off=off, w=w: e.tensor_scalar(
                    W[:, c, off:off + w], wst[k][:, 0:w], scale_ap[:, c:c + 1], None, ALU.mult),
                    reads=[B_wst[k], B_const], pwrites=[B_W])
                off += w

    def norm_transpose(i, cnt, tpb=(0, 0)):
        norm_part(i, cnt)
        return transpose_part(cnt, tpb)

    def norm_part(i, cnt):
        k = cnt % 2
        S.add("sp", lambda e: e.dma_start(out=xt[k][:], in_=xs[i * 128:(i + 1) * 128, :]),
              writes=[B_xt[k]], key="xt%d" % k)
        S.add("act", lambda e: e.activation(xjunk[:], xt[k][:], AF.Square, accum_out=stat[:, 0:1]),
              reads=[B_xt[k]], writes=[B_xjunk, B_stat])
        rstd_chain(stat[:, 0:1], stat[:, 1:2], stat[:, 2:3], 1.0 / D, B_stat)
        S.add("dve", lambda e: e.tensor_scalar(xb[k][:], xt[k][:], stat[:, 2:3], None, ALU.mult),
              reads=[B_xt[k], B_stat], writes=[B_xb[k]])

    def transpose_part(cnt, tpb=(0, 0)):
        k = cnt % 2
        tb = tpb[k]

        def tr(e):
            ins = None
            for c in range(16):
                ins = e.transpose(psb(tb, 2048)[:, c * 128:(c + 1) * 128], xb[k][:, c * 128:(c + 1) * 128], ident[:])
            return ins
        S.add("pe", tr, reads=[B_xb[k], B_const], writes=[B_ps[tb], B_ps[tb + 1]])
        S.add("act", lambda e: e.activation(uT[k][:].rearrange("p c t -> p (c t)"), psb(tb, 2048), AF.Copy),
              reads=[B_ps[tb], B_ps[tb + 1]], writes=[B_uT[k]])
        return k

    def rstd_chain(ss, tmp, rstd, inv_n, B):
        S.add("dve", lambda e: e.tensor_scalar(tmp, ss, inv_n, EPS, ALU.mult, ALU.add), reads=[B], writes=[B])
        S.add("act", lambda e: e.activation(tmp, tmp, AF.Ln), reads=[B], writes=[B])
        S.add("act", lambda e: e.activation(rstd, tmp, AF.Exp, scale=-0.5), reads=[B], writes=[B])

    def mm_feat(bank_ap, W, col0, n_m, rhs_of_c, start_first=True):
        def f(e):
            ins = None
            for c in range(16):
                ins = e.matmul(bank_ap, W[:, c, col0:col0 + n_m], rhs_of_c(c),
                               start=(c == 0 and start_first), stop=(c == 15), skip_group_check=True)
            return ins
        return f

    mark = apos[0]
    if True:
        Wkv = sb("Wkv", [128, 16, 2048], BF16)
        B_Wkv = Buf("Wkv")
        kst = [sb("kst%d" % k, [128, 8, 128], BF16) for k in range(2)]
        vst = [sb("vst%d" % k, [128, 1024], BF16) for k in range(2)]
        B_kst = [Buf("kst%d" % k) for k in range(2)]
        B_vst = [Buf("vst%d" % k) for k in range(2)]
        load_weights(Wkv, B_Wkv, [(1024, 3072)], npre_s)
        Wq = sb("Wq", [128, 16, 2048], BF16)
        B_Wq = Buf("Wq")
        qst = sb("qst", [128, 8, 128], BF16)
        B_qst = Buf("qst")
        ge = sb("ge", [128, 1024], F32)
        B_ge = Buf("ge")
        gst = sb("gst", [128, 1024], BF16)
        B_gst = Buf("gst")
        load_weights(Wq, B_Wq, [(0, 1024), (3072, 4096)], npre_s)
        norm_part(0, 0)
        norm_part(1, 1)
        transpose_part(0, (0, 6))
        for i in range(NB):
            k = i % 2
            if i + 1 < NB:
                transpose_part(i + 1, (0, 6))
            if i + 2 < NB:
                norm_part(i + 2, i)
            for pr in range(8):
                bk = 2 + pr // 4
                S.add("pe", mm_feat(psf(2, 1024)[:, pr * 128:(pr + 1) * 128], Wkv, pr * 128, 128,
                                    lambda c, k=k: uT[k][:, c, :], start_first=(pr % 4 == 0)),
                      reads=[B_Wkv, B_uT[k]], pwrites=[B_ps[bk]])
            S.add("act", lambda e, k=k: e.activation(kst[k][:].rearrange("p a t -> p (a t)"), psf(2, 1024), AF.Copy),
                  reads=[B_ps[2], B_ps[3]], writes=[B_kst[k]])
            S.add("pool", lambda e, k=k, i=i: e.dma_start(out=KT_d[:, i, :, :], in_=kst[k][:]),
                  reads=[B_kst[k]], writes=[B_KT[i]], key="kst%d" % k)
            for hf in range(2):
                def fv(e, hf=hf, k=k):
                    ins = None
                    for c in range(16):
                        ins = e.matmul(psf(4 + hf, 512), uT[k][:, c, :], Wkv[:, c, 1024 + hf * 512:1024 + (hf + 1) * 512],
                                       start=(c == 0), stop=(c == 15))
                    return ins
                S.add("pe", fv, reads=[B_Wkv, B_uT[k]], writes=[B_ps[4 + hf]])
            S.add("dve", lambda e, k=k: e.tensor_copy(vst[k][:], psf(4, 1024)),
                  reads=[B_ps[4], B_ps[5]], writes=[B_vst[k]])
            S.add("pool", lambda e, k=k, i=i: e.dma_start(out=V_d[:, i, :], in_=vst[k][:]),
                  reads=[B_vst[k]], writes=[B_V[i]], key="vst%d" % k)
            if i % 8 == 7:
                j = i // 8
                for pr in range(8):
                    bk = 2 + pr // 4
                    S.add("pe", mm_feat(psf(2, 1024)[:, pr * 128:(pr + 1) * 128], Wq, pr * 128, 128,
                                        lambda c, k=k: uT[k][:, c, :], start_first=(pr % 4 == 0)),
                          reads=[B_Wq, B_uT[k]], pwrites=[B_ps[bk]])
                S.add("act", lambda e: e.activation(qst[:].rearrange("p a t -> p (a t)"), psf(2, 1024), AF.Copy, scale=0.125),
                      reads=[B_ps[2], B_ps[3]], writes=[B_qst])
                S.add("pool", lambda e, j=j: e.dma_start(out=QT_d[:, j, :, :], in_=qst[:]),
                      reads=[B_qst], writes=[B_QT[j]], key="qst")
                for hf in range(2):
                    def fg(e, hf=hf, k=k):
                        ins = None
                        for c in range(16):
                            ins = e.matmul(psf(4 + hf, 512), uT[k][:, c, :], Wq[:, c, 1024 + hf * 512:1024 + (hf + 1) * 512],
                                           start=(c == 0), stop=(c == 15))
                        return ins
                    S.add("pe", fg, reads=[B_Wq, B_uT[k]], writes=[B_ps[4 + hf]])
                silu_from_psum(S, psf(4, 1024), [B_ps[4], B_ps[5]], ge, B_ge, gst, B_gst)
                S.add("pool", lambda e, j=j: e.dma_start(out=SG_d[:, j, :], in_=gst[:]),
                      reads=[B_gst], writes=[B_SG[j]], key="gst")
        p1_tail = [B_Wkv] + B_kst + B_vst + [B_Wq, B_qst, B_ge, B_gst]
    B_region = Buf("region")
    S.add("sp", lambda e: e.nop(), writes=p1_tail + [B_region])

    apos[0] = mark
    if True:
        W3 = sb("W3", [128, 16, 2560], BF16)
        B_W3 = Buf("W3")
        bias_s = sb("bias_s", [128, 16, 256], F32)
        mtmp = sb("mtmp", [128, 2, 256], F32)
        sink_s = sb("sink_s", [128, 16], F32)
        B_bias = Buf("bias")
        Ksw = sb("Ksw", [128, 4, 256], BF16)
        Vsw = sb("Vsw", [128, 2, 256], BF16)
        Qsw = sb("Qsw", [128, 8, 128], BF16)
        lg = sb("lg", [128, 4, 256], F32)
        pexp = sb("pexp", [128, 4, 256], BF16)
        pT = sb("pT", [128, 4, 2, 128], BF16)
        sm = sb("sm", [128, 32], F32)
        osw = sb("osw", [128, 1024], F32)
        ge3 = sb("ge3", [128, 1024], F32)
        sg3 = sb("sg3", [128, 1024], BF16)
        og3 = sb("og3", [128, 1024], BF16)
        B_Ksw, B_Vsw, B_Qsw, B_lg, B_pexp, B_pT, B_sm, B_osw, B_ge3, B_sg3, B_og3 = [
            Buf(n) for n in ("Ksw", "Vsw", "Qsw", "lg", "pexp", "pT", "sm", "osw", "ge3", "sg3", "og3")]
        S.add("sp", lambda e: e.nop(), reads=[B_region],
              writes=[B_W3, B_bias, B_Ksw, B_Vsw, B_Qsw, B_lg, B_pexp, B_pT, B_sm, B_osw, B_ge3, B_sg3, B_og3])
        load_weights(W3, B_W3, [(4096, 5120), (5120, 5376), (5376, 5632), (5632, 6656)], npre_s)
        W3kd = sb("W3kd", [128, 16, 4, 128], BF16)
        B_W3kd = Buf("W3kd")

        def fdup(e):
            ins = None
            for c in range(16):
                for half in range(2):
                    ins = e.tensor_copy(W3kd[:, c, :, half * 64:(half + 1) * 64],
                                        W3[:, c, 1024:1280].rearrange("p (g d) -> p g d", g=4))
            return ins
        S.add("pool", fdup, reads=[B_W3, B_region], writes=[B_W3kd])
        S.add("sp", lambda e: e.dma_start(out=bias_s[:].rearrange("p h c -> p (h c)"), in_=biasT), pwrites=[B_bias], key="bias")
        S.add("sp", lambda e: e.dma_start(out=mtmp[:, 0, :], in_=maskc), pwrites=[B_bias], key="bias")
        S.add("sp", lambda e: e.dma_start(out=mtmp[:, 1, :], in_=mask0), pwrites=[B_bias], key="bias")
        S.add("sp", lambda e: e.dma_start(out=sink_s[:], in_=sinkb), pwrites=[B_bias], key="bias")

        def bias_setup(e):
            ins = None
            for h in range(16):
                ins = e.tensor_tensor(bias_s[:, h, :], bias_s[:, h, :], mtmp[:, 0, :], ALU.add)
            return ins
        S.add("pool", bias_setup, reads=[B_bias], writes=[B_bias])

        norm_transpose(6, 0)
        norm_transpose(7, 1)
        for j in range(NS):
            for which, i in ((0, 8 * j + 6), (1, 8 * j + 7)):
                k = which
                for g in range(4):
                    def fk(e, g=g, k=k):
                        ins = None
                        for c in range(16):
                            ins = e.matmul(psf(2, 512)[:, g * 128:(g + 1) * 128], W3kd[:, c, g, :], uT[k][:, c, :],
                                           start=(c == 0 and g == 0), stop=(c == 15), skip_group_check=True)
                        return ins
                    S.add("pe", fk, reads=[B_W3kd, B_uT[k]], pwrites=[B_ps[2]])
                S.add("act", lambda e, which=which: e.activation(
                    Ksw[:, :, which * 128:(which + 1) * 128], psf(2, 512).rearrange("p (g t) -> p g t", g=4), AF.Copy),
                    reads=[B_ps[2]], pwrites=[B_Ksw])

                def fv3(e, k=k):
                    ins = None
                    for c in range(16):
                        ins = e.matmul(psf(3, 256), uT[k][:, c, :], W3[:, c, 1280:1536], start=(c == 0), stop=(c == 15))
                    return ins
                S.add("pe", fv3, reads=[B_W3, B_uT[k]], writes=[B_ps[3]])
                S.add("dve", lambda e, which=which: e.tensor_copy(Vsw[:, which, :], psf(3, 256)),
                      reads=[B_ps[3]], pwrites=[B_Vsw])
            for pr in range(8):
                bk = 4 + pr // 4
                S.add("pe", mm_feat(psf(4, 1024)[:, pr * 128:(pr + 1) * 128], W3, pr * 128, 128,
                                    lambda c, k=k: uT[k][:, c, :], start_first=(pr % 4 == 0)),
                      reads=[B_W3, B_uT[k]], pwrites=[B_ps[bk]])
            S.add("act", lambda e: e.activation(Qsw[:].rearrange("p a t -> p (a t)"), psf(4, 1024), AF.Copy, scale=0.125),
                  reads=[B_ps[4], B_ps[5]], writes=[B_Qsw])
            for hf in range(2):
                def fg3(e, hf=hf, k=k):
                    ins = None
                    for c in range(16):
                        ins = e.matmul(psf(6 + hf, 512), uT[k][:, c, :], W3[:, c, 1536 + hf * 512:1536 + (hf + 1) * 512],
                                       start=(c == 0), stop=(c == 15))
                    return ins
                S.add("pe", fg3, reads=[B_W3, B_uT[k]], writes=[B_ps[6 + hf]])
            silu_from_psum(S, psf(6, 1024), [B_ps[6], B_ps[7]], ge3, B_ge3, sg3, B_sg3)
            if j + 1 < NS:
                norm_transpose(8 * (j + 1) + 6, 0)
                norm_transpose(8 * (j + 1) + 7, 1)
            btab = bias_s
            PERM = (0, 2, 1, 3)
            for g in range(4):
                def flog(e, g=g):
                    ins = None
                    for s_ in range(4):
                        h = 4 * g + PERM[s_]
                        pb = (h % 2) * 64
                        ins = e.matmul(psf(2, 1024)[:, s_ * 256:(s_ + 1) * 256],
                                       Qsw[pb:pb + 64, h // 2, :], Ksw[pb:pb + 64, g, :],
                                       start=(s_ % 2 == 0), stop=True, skip_group_check=True)
                    return ins
                S.add("pe", flog, reads=[B_Qsw, B_Ksw], writes=[B_ps[2], B_ps[3]])
                S.add("dve", lambda e, g=g, btab=btab: e.tensor_tensor(
                    lg[:], psf(2, 1024).rearrange("p (h c) -> p h c", h=4), btab[:, 4 * g:4 * g + 4, :], ALU.add),
                    reads=[B_ps[2], B_ps[3], B_bias], writes=[B_lg])
                if j == 0:
                    def fm0(e):
                        ins = None
                        for s_ in range(4):
                            ins = e.tensor_tensor(lg[:, s_, :], lg[:, s_, :], mtmp[:, 1, :], ALU.add)
                        return ins
                    S.add("dve", fm0, reads=[B_bias], writes=[B_lg])
                S.add("dve", lambda e: e.tensor_reduce(sm[:, 0:4], lg[:], AX.X, ALU.max), reads=[B_lg], writes=[B_sm])
                S.add("dve", lambda e, g=g: e.tensor_tensor(sm[:, 0:4], sm[:, 0:4], sink_s[:, 4 * g:4 * g + 4], ALU.max),
                      reads=[B_sm, B_bias], writes=[B_sm])
                S.add("dve", lambda e: e.tensor_scalar(sm[:, 4:8], sm[:, 0:4], -1.0, None, ALU.mult),
                      reads=[B_sm], writes=[B_sm])
                S.add("dve", lambda e, g=g: e.tensor_tensor(sm[:, 12:16], sink_s[:, 4 * g:4 * g + 4], sm[:, 0:4], ALU.subtract),
                      reads=[B_sm, B_bias], writes=[B_sm])

                def fexp(e):
                    ins = None
                    for hh in range(4):
                        ins = e.activation(pexp[:, hh, :], lg[:, hh, :], AF.Exp, bias=sm[:, 4 + hh:5 + hh],
                                           accum_out=sm[:, 8 + hh:9 + hh])
                    ins = e.activation(sm[:, 12:16], sm[:, 12:16], AF.Exp)
                    return ins
                S.add("act", fexp, reads=[B_lg, B_sm], writes=[B_pexp, B_sm])
                S.add("dve", lambda e: e.tensor_tensor(sm[:, 8:12], sm[:, 8:12], sm[:, 12:16], ALU.add),
                      reads=[B_sm], writes=[B_sm])
                S.add("dve", lambda e: e.reciprocal(sm[:, 16:20], sm[:, 8:12]), reads=[B_sm], writes=[B_sm])

                def ftr(e):
                    ins = None
                    for hh in range(4):
                        for half in range(2):
                            ins = e.transpose(psb(0, 1024)[:, (hh * 2 + half) * 128:(hh * 2 + half + 1) * 128],
                                              pexp[:, hh, half * 128:(half + 1) * 128], ident[:])
                    return ins
                S.add("pe", ftr, reads=[B_pexp, B_const], writes=[B_ps[0]])
                S.add("act", lambda e: e.activation(pT[:].rearrange("p h a t -> p (h a t)"), psb(0, 1024), AF.Copy),
                      reads=[B_ps[0]], writes=[B_pT])

                def fpv(e, g=g):
                    ins = None
                    for hh in range(4):
                        for half in range(2):
                            ins = e.matmul(psf(1, 256)[:, hh * 64:(hh + 1) * 64], pT[:, hh, half, :],
                                           Vsw[:, half, g * 64:(g + 1) * 64],
                                           start=(hh == 0 and half == 0), stop=(half == 1), skip_group_check=True)
                    return ins
                S.add("pe", fpv, reads=[B_pT, B_Vsw], writes=[B_ps[1]])

                def fnorm(e, g=g):
                    ins = None
                    for s_ in range(4):
                        h = 4 * g + PERM[s_]
                        ins = e.tensor_scalar(osw[:, h * 64:(h + 1) * 64], psf(1, 256)[:, s_ * 64:(s_ + 1) * 64],
                                              sm[:, 16 + s_:17 + s_], None, ALU.mult)
                    return ins
                S.add("dve", fnorm, reads=[B_ps[1], B_sm], pwrites=[B_osw])
            S.add("act", lambda e: e.activation(ge3[:], osw[:], AF.Square, accum_out=sm[:, 20:21]),
                  reads=[B_osw], writes=[B_ge3, B_sm])
            rstd_chain(sm[:, 20:21], sm[:, 21:22], sm[:, 22:23], 1.0 / 1024, B_sm)
            S.add("dve", lambda e: e.scalar_tensor_tensor(og3[:], osw[:], sm[:, 22:23], sg3[:], ALU.mult, ALU.mult),
                  reads=[B_osw, B_sm, B_sg3], writes=[B_og3])
            S.add("pool", lambda e, j=j: e.dma_start(out=OGW_d[:, j, :], in_=og3[:]),
                  reads=[B_og3], writes=[B_OGW[j]], key="og3")
        p3_tail = [B_W3, B_W3kd, B_bias, B_Ksw, B_Vsw, B_Qsw, B_lg, B_pexp, B_pT, B_sm, B_osw, B_ge3, B_sg3, B_og3]
    S.add("sp", lambda e: e.nop(), writes=p3_tail + B_xt + [B_xjunk, B_stat] + B_xb + B_uT + B_wst + [B_region])

    CH = 2
    apos[0] = mark
    if True:
        sb4 = sb
        Wo = sb4("Wo", [128, 16, 2048], BF16)
        B_Wo = Buf("Wo")
        npost_s = sb4("npost_s", [128, D], F32)
        KA = [sb4("KA%d" % k, [128, CH, 8, 128], BF16) for k in range(2)]
        KB = [sb4("KB%d" % k, [128, CH, 8, 128], BF16) for k in range(2)]
        VC = [sb4("VC%d" % k, [128, CH, 1024], BF16) for k in range(2)]
        B_KA = [Buf("KA%d" % k) for k in range(2)]
        B_KB = [Buf("KB%d" % k) for k in range(2)]
        B_VC = [Buf("VC%d" % k) for k in range(2)]
        QA = sb4("QA", [128, 8, 128], BF16)
        QB = sb4("QB", [128, 8, 128], BF16)
        B_QAd, B_QAr, B_QBd, B_QBr = Buf("QAd"), Buf("QAr"), Buf("QBd"), Buf("QBr")
        eb = [sb4("eb%d" % x, [128, 1024], F32) for x in range(2)]
        spb = [sb4("spb%d" % x, [128, 1024], BF16) for x in range(2)]
        Ab = [sb4("Ab%d" % x, [128, 1024], BF16) for x in range(2)]
        B_eb = [Buf("eb%d" % x) for x in range(2)]
        B_spb = [Buf("spb%d" % x) for x in range(2)]
        B_Ab = [Buf("Ab%d" % x) for x in range(2)]
        osb = sb4("osb", [128, 8, 2, 64], F32)
        B_osb = Buf("osb")
        og = sb4("og", [128, 2048], BF16)
        B_og = Buf("og")
        sgl = sb4("sgl", [128, 1024], BF16)
        B_sgl = Buf("sgl")
        ogT = uT[0]
        B_ogT = Buf("ogT")
        xo = xt[0]
        B_xo = Buf("xo")
        yb = xt[1]
        B_yb = Buf("yb")
        jk = xjunk
        B_jk = Buf("jk")
        st4 = sb4("st4", [128, 8], F32)
        B_st4 = Buf("st4")
        all4 = [B_Wo] + B_KA + B_KB + B_VC + [B_QAd, B_QAr, B_QBd, B_QBr] + B_eb + B_spb + B_Ab + \
            [B_osb, B_og, B_sgl, B_ogT, B_xo, B_yb, B_jk, B_st4]
        S.add("sp", lambda e: e.nop(), reads=[B_region], writes=all4)
        wst4 = wst
        B_wst4 = [Buf("wst4_%d" % k) for k in range(2)]
        for c in range(16):
            k = c % 2
            S.add("sp", lambda e, k=k, c=c: e.dma_start(out=wst4[k][:], in_=w_out[c * 128:(c + 1) * 128, :]),
                  writes=[B_wst4[k]], key="wst4_%d" % k)
            S.add("dve", lambda e, k=k, c=c: e.tensor_scalar(Wo[:, c, :], wst4[k][:], gsbw_s[:, c:c + 1], None, ALU.mult),
                  reads=[B_wst4[k], B_const], pwrites=[B_Wo])
        S.add("sp", lambda e: e.dma_start(out=npost_s[:], in_=npost), pwrites=[B_Wo], key="npost")

        Rq = [sb4("Rq%d" % x, [128, 1024], BF16) for x in range(2)]
        sel = sb4("sel", [128, 2, 128], BF16)

        def kconst(e):
            ins = None
            for k in range(2):
                e.memset(KA[k][64:128], 0.0)
                e.memset(KB[k][0:64], 0.0)
            e.memset(QA[64:128], 0.0)
            e.memset(QB[0:64], 0.0)
            for x in range(2):
                e.memset(Rq[x][:], 0.0)
            e.memset(sel[:], 0.0)
            e.memset(sel[64:65, 0, :], 1.0)
            ins = e.memset(sel[32:33, 1, :], 1.0)
            return ins
        S.add("pool", kconst, writes=B_KA + B_KB + [B_QAr, B_QBr, B_QAd, B_QBd])

        ZB = [[B_ps[0], B_ps[1]], [B_ps[2], B_ps[3]]]
        RB = [B_ps[4], B_ps[5]]
        B_R = [Buf("R_A"), Buf("R_B")]
        OB = [B_ps[6], B_ps[7]]
        Qt = [QA, QB]
        Kt = [KA, KB]
        KR = [65, 128]
        AUG = [64, 32]
        B_Qd = [B_QAd, B_QBd]
        B_Qr = [B_QAr, B_QBr]
        B_Kt = [B_KA, B_KB]
        nchunk = 0

        def z_step(X, kb, bi, first):

            def f(e):
                ins = None
                for h in range(8):
                    ins = e.matmul(psf(2 * X, 1024)[:, h * 128:(h + 1) * 128], Kt[X][kb][:, bi, h, :],
                                   Qt[X][:, h, :],
                                   start=(h % 4 == 0), stop=False, skip_group_check=True)
                if first:
                    for k2 in range(2):
                        ins = e.matmul(psf(2 * X + k2, 512), ident[:], dmask[:, k2 * 512:(k2 + 1) * 512],
                                       start=False, stop=False, skip_group_check=True)
                return ins
            S.add("pe", f, reads=[B_Kt[X][kb], B_Qd[X], B_const], writes=ZB[X])

        for j in range(NS):
            nblk = 8 * j + 8
            S.add("sp", lambda e, j=j: e.dma_start(out=QA[0:64], in_=QT_d[0:64, j, :, :]),
                  reads=[B_QT[j]], writes=[B_QAd], key="QAd")
            S.add("sp", lambda e, j=j: e.dma_start(out=QB[64:128], in_=QT_d[64:128, j, :, :]),
                  reads=[B_QT[j]], writes=[B_QBd], key="QBd")
            blocks = list(range(nblk - 1, -1, -1))
            chunks = [blocks[a:a + CH] for a in range(0, nblk, CH)]

            def load_chunk(ch, kb):
                lo = ch[-1]
                n = len(ch)
                S.add("sp", lambda e: e.dma_start(out=KA[kb][0:64, 0:n], in_=KT_d[0:64, lo:lo + n, :, :]),
                      reads=[B_KT[b] for b in ch], writes=[B_KA[kb]], key="KA%d" % kb)
                S.add("sp", lambda e: e.dma_start(out=KB[kb][64:128, 0:n], in_=KT_d[64:128, lo:lo + n, :, :]),
                      reads=[B_KT[b] for b in ch], writes=[B_KB[kb]], key="KB%d" % kb)
                S.add("sp", lambda e: e.dma_start(out=VC[kb][:, 0:n], in_=V_d[:, lo:lo + n, :]),
                      reads=[B_V[b] for b in ch], writes=[B_VC[kb]], key="VC%d" % kb)

            kb0 = nchunk % 2
            load_chunk(chunks[0], kb0)
            seq = [(ci, b) for ci, ch in enumerate(chunks) for b in ch]
            for X in range(2):
                z_step(X, kb0, seq[0][1] - chunks[0][-1], True)
            for si, (ci, b) in enumerate(seq):
                kb = (nchunk + ci) % 2
                bi = b - chunks[ci][-1]
                first = (si == 0)
                last = (si == len(seq) - 1)
                if bi == len(chunks[ci]) - 1 and ci + 1 < len(chunks):
                    load_chunk(chunks[ci + 1], (nchunk + ci + 1) % 2)
                for X in range(2):
                    S.add("act", lambda e, X=X: e.activation(eb[X][:], psf(2 * X, 1024), AF.Exp),
                          reads=ZB[X], writes=[B_eb[X]])
                if not first:
                    for X in range(2):
                        def fradd(e, X=X):
                            ins = None
                            for k2 in range(2):
                                ins = e.matmul(psf(2 * X + k2, 512), sel[:, X, :], Rq[X][:, k2 * 512:(k2 + 1) * 512],
                                               start=False, stop=False, skip_group_check=True)
                            return ins
                        S.add("pe", fradd, reads=[B_const, B_Qr[X]], writes=ZB[X])
                for X in range(2):
                    S.add("act", lambda e, X=X: e.activation(spb[X][:], eb[X][:], AF.Ln, bias=1.0),
                          reads=[B_eb[X]], writes=[B_spb[X]])
                for X in range(2):
                    def ftri(e, X=X):
                        ins = None
                        for k2 in range(2):
                            ins = e.matmul(psf(2 * X + k2, 512), tri[:], spb[X][:, k2 * 512:(k2 + 1) * 512],
                                           start=False, stop=True, skip_group_check=True)
                        return ins
                    S.add("pe", ftri, reads=[B_spb[X], B_const], writes=ZB[X])
                    if not last:
                        def ferow(e, X=X, first=first):
                            ins = None
                            for k2 in range(2):
                                mx = 65 if X == 0 else 33
                                ins = e.matmul(psf(4 + k2, 512)[0:mx, :], erow[:, 128 * X:128 * X + mx],
                                               spb[X][:, k2 * 512:(k2 + 1) * 512],
                                               start=(first and X == 0), stop=False, skip_group_check=True)
                            return ins
                        S.add("pe", ferow, reads=[B_spb[X], B_const], writes=[B_R[X]])
                        a = AUG[X]
                        S.add("dve", lambda e, X=X, a=a: e.tensor_copy(
                            Rq[X][a:a + 1, :], psf(4, 1024)[a:a + 1, :]),
                            reads=[B_R[X]], writes=[B_Qr[X]])
                for X in range(2):
                    S.add("act", lambda e, X=X: e.activation(Ab[X][:], psf(2 * X, 1024), AF.Exp),
                          reads=ZB[X], writes=[B_Ab[X]])
                for X in range(2):
                    if not last:
                        ci2, b2 = seq[si + 1]
                        z_step(X, (nchunk + ci2) % 2, b2 - chunks[ci2][-1], False)

                    def fav(e, X=X, kb=kb, bi=bi, first=first, last=last):
                        ins = None
                        for h in range(8):
                            hd = 2 * h + X
                            ins = e.matmul(psf(6 + X, 512)[:, h * 64:(h + 1) * 64], Ab[X][:, h * 128:(h + 1) * 128],
                                           VC[kb][:, bi, hd * 64:(hd + 1) * 64],
                                           start=(first and h == 0), stop=last, skip_group_check=True)
                        return ins
                    S.add("pe", fav, reads=[B_Ab[X], B_VC[kb]], writes=[OB[X]])
            nchunk += len(chunks)
            for X in range(2):
                S.add("dve", lambda e, X=X: e.tensor_copy(osb[:, :, X, :], psf(6 + X, 512).rearrange("p (h d) -> p h d", h=8)),
                      reads=[OB[X]], pwrites=[B_osb])
            osb2 = osb[:].rearrange("p h x d -> p (h x d)")
            S.add("sp", lambda e, j=j: e.dma_start(out=sgl[:], in_=SG_d[:, j, :]), reads=[B_SG[j]], writes=[B_sgl], key="sgl")
            S.add("sp", lambda e, j=j: e.dma_start(out=og[:, 1024:2048], in_=OGW_d[:, j, :]), reads=[B_OGW[j]],
                  pwrites=[B_og], key="og")
            S.add("sp", lambda e, j=j: e.dma_start(out=xo[:], in_=xs[(8 * j + 7) * 128:(8 * j + 8) * 128, :]),
                  writes=[B_xo], key="xo")
            S.add("act", lambda e: e.activation(jk[:, 0:1024], osb2, AF.Square, accum_out=st4[:, 0:1]),
                  reads=[B_osb], writes=[B_jk, B_st4])
            rstd_chain(st4[:, 0:1], st4[:, 1:2], st4[:, 2:3], 1.0 / 1024, B_st4)
            S.add("dve", lambda e: e.scalar_tensor_tensor(og[:, 0:1024], osb2, st4[:, 2:3], sgl[:], ALU.mult, ALU.mult),
                  reads=[B_osb, B_st4, B_sgl], pwrites=[B_og])

            def ftr4(e):
                ins = None
                for c in range(16):
                    ins = e.transpose(psb(0, 2048)[:, c * 128:(c + 1) * 128], og[:, c * 128:(c + 1) * 128], ident[:])
                return ins
            S.add("pe", ftr4, reads=[B_og, B_const], writes=[B_ps[0], B_ps[1]])
            S.add("act", lambda e: e.activation(ogT[:].rearrange("p c t -> p (c t)"), psb(0, 2048), AF.Copy),
                  reads=[B_ps[0], B_ps[1]], writes=[B_ogT])
            for q4 in range(4):
                def fy(e, q4=q4):
                    ins = None
                    for c in range(16):
                        ins = e.matmul(psf(q4, 512), ogT[:, c, :], Wo[:, c, q4 * 512:(q4 + 1) * 512],
                                       start=(c == 0), stop=(c == 15))
                    return ins
                S.add("pe", fy, reads=[B_ogT, B_Wo], writes=[B_ps[q4]])
            S.add("act", lambda e: e.activation(yb[:], psf(0, 2048), AF.Square, accum_out=st4[:, 3:4]),
                  reads=[B_ps[0], B_ps[1], B_ps[2], B_ps[3]], writes=[B_yb, B_st4])
            rstd_chain(st4[:, 3:4], st4[:, 4:5], st4[:, 5:6], 1.0 / D, B_st4)
            S.add("dve", lambda e: e.scalar_tensor_tensor(yb[:], psf(0, 2048), st4[:, 5:6], npost_s[:], ALU.mult, ALU.mult),
                  reads=[B_ps[0], B_ps[1], B_ps[2], B_ps[3], B_st4, B_Wo], writes=[B_yb])
            S.add("pool", lambda e: e.tensor_tensor(yb[:], yb[:], xo[:], ALU.add), reads=[B_xo], writes=[B_yb])
            S.add("sp", lambda e, j=j: e.dma_start(out=out[j * 128:(j + 1) * 128, :], in_=yb[:]),
                  reads=[B_yb], writes=[Buf("out%d" % j)], key="yb")
        S.add("sp", lambda e: e.nop(), writes=all4 + B_wst4 + [Buf("end")])
        S.emit(nc, st)
    st.close()
    return nc


def silu_from_psum(S, z_ps, B_z, ge, B_ge, gout, B_gout):
    S.add("act", lambda e: e.activation(ge[:], z_ps, AF.Exp, scale=-1.0), reads=B_z, writes=[B_ge])
    S.add("dve", lambda e: e.tensor_scalar(ge[:], ge[:], 1.0, None, ALU.add), reads=[B_ge], writes=[B_ge])
    S.add("dve", lambda e: e.reciprocal(ge[:], ge[:]), reads=[B_ge], writes=[B_ge])
    S.add("dve", lambda e: e.tensor_tensor(gout[:], z_ps, ge[:], ALU.mult), reads=B_z + [B_ge], writes=[B_gout])


def _t5_bucket_table():
    qi = np.arange(128)[:, None]
    ci = np.arange(256)[None, :]
    dist = qi + 128 - ci
    n = np.maximum(dist, 0)
    nf = np.maximum(n, 1).astype(np.float32)
    large = 16 + (np.log(nf / np.float32(16)) / np.float32(np.log(128 / 16)) * np.float32(16)).astype(np.int32)
    large = np.minimum(large, 31)
    bucket = np.where(n < 16, n, large)
    valid = (dist >= 0) & (dist < 128)
    return bucket, valid


def host_inputs(x, w_in, w_out, norm_pre, norm_post, gn_sb, gn_sw, sinks, rel_bias, NS):
    bf = ml_dtypes.bfloat16
    NB = 8 * NS
    x2 = np.ascontiguousarray(x.reshape(-1, D)).astype(np.float32, copy=False)
    ntile = x2.shape[0] // 128
    assert ntile == NB
    bucket, valid = _t5_bucket_table()
    bias = rel_bias.astype(np.float32)[bucket]
    hord = [4 * g + p for g in range(4) for p in (0, 2, 1, 3)]
    biasT = np.ascontiguousarray(bias[:, :, hord].transpose(0, 2, 1)).reshape(128, 16 * 256)
    maskc = np.where(valid, 0.0, NEG).astype(np.float32)
    ident = np.eye(128, dtype=np.float32).astype(bf)
    jj = np.arange(128)[:, None]
    ss = np.arange(128)[None, :]
    tri = np.where(jj >= ss, -1.0, 0.0).astype(np.float32).astype(bf)
    erow = np.zeros((128, 256), np.float32)
    erow[:, 64] = -1.0
    erow[:, 128 + 32] = -1.0
    erow = erow.astype(bf)
    dm = np.where(jj < ss, 0.0, NEG).astype(np.float32)
    dmask = np.tile(dm, (1, 8)).astype(bf)
    common = dict(
        w_in=np.ascontiguousarray(w_in.reshape(D, DIN)), w_out=np.ascontiguousarray(w_out.reshape(D, D)),
        npre=np.ascontiguousarray(norm_pre.reshape(16, 128).T),
        gsbw=np.ascontiguousarray(np.concatenate([gn_sb.reshape(8, 128), gn_sw.reshape(8, 128)], 0).T),
        npost=np.ascontiguousarray(np.broadcast_to(norm_post.reshape(1, D), (128, D))),
        sinkb=np.ascontiguousarray(np.broadcast_to(sinks.reshape(16)[hord].reshape(1, 16), (128, 16))),
        biasT=biasT, maskc=maskc, ident=ident, tri=tri, erow=erow, dmask=dmask)
    in_maps = []
    for c in range(NCORES):
        pad = 7 - c
        xs = np.zeros((NB * 128, D), np.float32)
        n_real = (NB - pad) * 128
        xs[pad * 128:] = x2[:n_real]
        m0 = np.zeros((128, 256), np.float32)
        if c == 0:
            m0[:, :128] = NEG
        d = dict(common)
        d["xs"] = xs
        d["mask0"] = m0
        in_maps.append(d)
    return in_maps


def run(inputs, NS):
    x = np.asarray(inputs["x"])
    in_maps = host_inputs(x, *[np.asarray(inputs[k], dtype=np.float32) for k in
                               ("w_in", "w_out", "norm_pre", "norm_post", "gn_sb", "gn_sw", "sinks", "rel_bias")], NS=NS)
    nc = build_nc(NS)
    res = run_bass_kernel_spmd(nc, in_maps, core_ids=list(range(NCORES)))
    outs = [np.asarray(r["out"]).reshape(NS, 128, D) for r in res.results]
    full = np.stack(outs, axis=1)
    return full.reshape(1, NS * 8 * 128, D).astype(np.float32)


def kernel(x, w_in, w_out, norm_pre, norm_post, gn_sb, gn_sw, sinks, rel_bias):
    return run(dict(x=x, w_in=w_in, w_out=w_out, norm_pre=norm_pre, norm_post=norm_post, gn_sb=gn_sb,
                    gn_sw=gn_sw, sinks=sinks, rel_bias=rel_bias), NS=16)
```

```python
import numpy as np
import ml_dtypes
from contextlib import ExitStack
import concourse.bass as bass
import concourse.mybir as mybir
from concourse.bass_utils import run_bass_kernel_spmd

F32 = mybir.dt.float32
BF16 = mybir.dt.bfloat16
AF = mybir.ActivationFunctionType
ALU = mybir.AluOpType
AX = mybir.AxisListType

D = 2048
DIN = 6656
NCORES = 8
NEG = -30000.0
EPS = 1e-6


class Buf:
    def __init__(self, name):
        self.name = name
        self.writers = []
        self.readers = []
        self.prev = set()
        self.gen_open = False


class Op:
    __slots__ = ("eng", "fn", "deps", "idx", "key", "token", "need")

    def __init__(self, eng, fn, deps, idx, key):
        self.eng, self.fn, self.deps, self.idx, self.key = eng, fn, deps, idx, key
        self.token = None
        self.need = False


class Sched:
    ENGS = ("pe", "act", "dve", "pool", "sp")

    def __init__(self):
        self.ops = []

    def add(self, eng, fn, reads=(), writes=(), pwrites=(), key=None):
        idx = len(self.ops)
        deps = set()
        for b in reads:
            deps.update(b.writers)
        for b in writes:
            prev = set(b.writers) | set(b.readers)
            deps.update(prev)
            b.prev, b.writers, b.readers, b.gen_open = prev, [idx], [], False
        for b in pwrites:
            if b.gen_open and not b.readers:
                deps.update(b.prev)
                b.writers.append(idx)
            else:
                prev = set(b.writers) | set(b.readers)
                deps.update(prev)
                b.prev, b.writers, b.readers, b.gen_open = prev, [idx], [], True
        for b in reads:
            b.readers.append(idx)
        deps.discard(idx)
        self.ops.append(Op(eng, fn, deps, idx, key))
        return idx

    def emit(self, nc, stack):
        ops = self.ops
        for op in ops:
            for d in op.deps:
                dop = ops[d]
                if dop.eng == "pe" and op.eng == "pe" and dop.key is None and op.key is None:
                    continue
                dop.need = True
        sems = {}

        def sem(name):
            if name not in sems:
                sems[name] = stack.enter_context(nc.semaphore("s_" + name))
            return sems[name]

        counts = {}
        for op in ops:
            if op.key is not None:
                k = "d_" + op.key
                counts[k] = counts.get(k, 0) + 16
                op.token = (k, counts[k])
            elif op.need:
                k = "e_" + op.eng
                counts[k] = counts.get(k, 0) + 1
                op.token = (k, counts[k])
        per = {e: [] for e in self.ENGS}
        for op in ops:
            per[op.eng].append(op)
        block = stack.enter_context(nc.Block())

        def body(engname):
            def run(eng):
                waited = {}
                for op in per[engname]:
                    need = {}
                    for d in op.deps:
                        dop = ops[d]
                        if dop.token is None:
                            continue
                        if dop.eng == "pe" and engname == "pe" and dop.key is None and op.key is None:
                            continue
                        k, v = dop.token
                        if need.get(k, 0) < v:
                            need[k] = v
                    for k, v in need.items():
                        if waited.get(k, 0) >= v:
                            continue
                        eng.wait_ge(sem(k), v)
                        waited[k] = v
                    ins = op.fn(eng)
                    if op.token is not None:
                        k, v = op.token
                        ins.then_inc(sem(k), 16 if op.key is not None else 1)
                if engname == "sp":
                    for k, v in counts.items():
                        if waited.get(k, 0) < v:
                            eng.wait_ge(sem(k), v)
            return run

        block.tensor(body("pe"))
        block.scalar(body("act"))
        block.vector(body("dve"))
        block.gpsimd(body("pool"))
        block.sync(body("sp"))


def build_nc(NS):
    NB = 8 * NS
    nc = bass.Bass("TRN2", target_bir_lowering=False)
    S = Sched()
    st = ExitStack()

    def dram_in(name, shape, dt=F32):
        return nc.dram_tensor(name, list(shape), dt, kind="ExternalInput").ap()

    xs = dram_in("xs", [NB * 128, D])
    w_in = dram_in("w_in", [D, DIN])
    w_out = dram_in("w_out", [D, D])
    npre = dram_in("npre", [128, 16])
    gsbw = dram_in("gsbw", [128, 16])
    npost = dram_in("npost", [128, D])
    sinkb = dram_in("sinkb", [128, 16])
    biasT = dram_in("biasT", [128, 16 * 256])
    maskc = dram_in("maskc", [128, 256])
    mask0 = dram_in("mask0", [128, 256])
    ident_d = dram_in("ident", [128, 128], BF16)
    tri_d = dram_in("tri", [128, 128], BF16)
    erow_d = dram_in("erow", [128, 256], BF16)
    dmask_d = dram_in("dmask", [128, 1024], BF16)
    sel_d = dram_in("sel", [128, 256], BF16)
    out = nc.dram_tensor("out", [NS * 128, D], F32, kind="ExternalOutput").ap()

    KT_d = nc.dram_tensor("KT_d", [128, NB, 8, 128], BF16).ap()
    V_d = nc.dram_tensor("V_d", [128, NB, 1024], BF16).ap()
    QT_d = nc.dram_tensor("QT_d", [128, NS, 8, 128], BF16).ap()
    SG_d = nc.dram_tensor("SG_d", [128, NS, 1024], BF16).ap()
    OGW_d = nc.dram_tensor("OGW_d", [128, NS, 1024], BF16).ap()
    B_KT = [Buf("KT_d%d" % i) for i in range(NB)]
    B_V = [Buf("V_d%d" % i) for i in range(NB)]
    B_QT = [Buf("QT_d%d" % j) for j in range(NS)]
    B_SG = [Buf("SG_d%d" % j) for j in range(NS)]
    B_OGW = [Buf("OGW_d%d" % j) for j in range(NS)]

    ARENA_BYTES = 204 * 1024
    arena = st.enter_context(nc.sbuf_tensor("arena", [128, ARENA_BYTES // 2], BF16))
    apos = [0]

    class _T:
        def __init__(self, ap):
            self.ap = ap

        def __getitem__(self, key):
            return self.ap[key]

    def sb(name, shape, dt):
        nfree = 1
        for s_ in shape[1:]:
            nfree *= s_
        nbytes = nfree * (4 if dt == F32 else 2)
        nbytes = (nbytes + 63) // 64 * 64
        off = apos[0]
        apos[0] += nbytes
        assert apos[0] <= ARENA_BYTES, (name, apos[0])
        v = arena[:, off // 2:(off + nbytes) // 2]
        if dt == F32:
            v = v.bitcast(F32)
        v = v[:, 0:nfree]
        if len(shape) == 3:
            v = v.rearrange("p (a b) -> p a b", a=shape[1])
        elif len(shape) == 4:
            v = v.rearrange("p (a b c) -> p a b c", a=shape[1], b=shape[2])
        return _T(v)

    ps = st.enter_context(nc.psum_tensor("ps", [128, 8 * 512], F32))
    B_ps = [Buf("bank%d" % k) for k in range(8)]

    def psf(b0, ncols):
        return ps[:, b0 * 512: b0 * 512 + ncols]

    def psb(b0, ncols):
        return ps[:, b0 * 512: b0 * 512 + ncols // 2].bitcast(BF16)

    ident = sb("ident", [128, 128], BF16)
    tri = sb("tri", [128, 128], BF16)
    erow = sb("erow", [128, 256], BF16)
    dmask = sb("dmask", [128, 1024], BF16)
    npre_s = sb("npre_s", [128, 16], F32)
    gsbw_s = sb("gsbw_s", [128, 16], F32)
    B_const = Buf("const")

    def ld_const(dst, src):
        S.add("sp", lambda e: e.dma_start(out=dst, in_=src), pwrites=[B_const], key="const")
    ld_const(ident[:], ident_d)
    ld_const(tri[:], tri_d)
    ld_const(erow[:], erow_d)
    ld_const(dmask[:], dmask_d)
    ld_const(npre_s[:], npre)
    ld_const(gsbw_s[:], gsbw)

    xt = [sb("xt%d" % k, [128, D], F32) for k in range(2)]
    B_xt = [Buf("xt%d" % k) for k in range(2)]
    xjunk = sb("xjunk", [128, D], BF16)
    B_xjunk = Buf("xjunk")
    xb = [sb("xb%d" % k, [128, D], BF16) for k in range(2)]
    B_xb = [Buf("xb%d" % k) for k in range(2)]
    uT = [sb("uT%d" % k, [128, 16, 128], BF16) for k in range(2)]
    B_uT = [Buf("uT%d" % k) for k in range(2)]
    stat = sb("stat", [128, 8], F32)
    B_stat = Buf("stat")
    wst = [sb("wst%d" % k, [128, 2048], F32) for k in range(2)]
    B_wst = [Buf("wst%d" % k) for k in range(2)]

    def load_weights(W, B_W, colranges, scale_ap, nchunks=16, src=None):
        src = w_in if src is None else src
        n = 0
        for c in range(nchunks):
            off = 0
            for (c0, c1) in colranges:
                w = c1 - c0
                k = n % 2
                n += 1
                S.add("sp", lambda e, k=k, c=c, c0=c0, c1=c1, w=w: e.dma_start(
                    out=wst[k][:, 0:w], in_=src[c * 128:(c + 1) * 128, c0:c1]),
                    writes=[B_wst[k]], key="wst%d" % k)
                S.add("dve", lambda e, k=k, c=c, off=off, w=w: e.tensor_scalar(
                    W[:, c, off:off + w], wst[k][:, 0:w], scale_ap[:, c:c + 1], None, ALU.mult),
                    reads=[B_wst[k], B_const], pwrites=[B_W])
                off += w

    def norm_transpose(i, cnt, tpb=(0, 0)):
        norm_part(i, cnt)
        return transpose_part(cnt, tpb)

    def norm_part(i, cnt):
        k = cnt % 2
        S.add("sp", lambda e: e.dma_start(out=xt[k][:], in_=xs[i * 128:(i + 1) * 128, :]),
              writes=[B_xt[k]], key="xt%d" % k)
        S.add("act", lambda e: e.activation(xjunk[:], xt[k][:], AF.Square, accum_out=stat[:, 0:1]),
              reads=[B_xt[k]], writes=[B_xjunk, B_stat])
        rstd_chain(stat[:, 0:1], stat[:, 1:2], stat[:, 2:3], 1.0 / D, B_stat)
        S.add("dve", lambda e: e.tensor_scalar(xb[k][:], xt[k][:], stat[:, 2:3], None, ALU.mult),
              reads=[B_xt[k], B_stat], writes=[B_xb[k]])

    def transpose_part(cnt, tpb=(0, 0)):
        k = cnt % 2
        tb = tpb[k]

        def tr(e):
            ins = None
            for c in range(16):
                ins = e.transpose(psb(tb, 2048)[:, c * 128:(c + 1) * 128], xb[k][:, c * 128:(c + 1) * 128], ident[:])
            return ins
        S.add("pe", tr, reads=[B_xb[k], B_const], writes=[B_ps[tb], B_ps[tb + 1]])
        S.add("act", lambda e: e.activation(uT[k][:].rearrange("p c t -> p (c t)"), psb(tb, 2048), AF.Copy),
              reads=[B_ps[tb], B_ps[tb + 1]], writes=[B_uT[k]])
        return k

    def rstd_chain(ss, tmp, rstd, inv_n, B):
        S.add("dve", lambda e: e.tensor_scalar(tmp, ss, inv_n, EPS, ALU.mult, ALU.add), reads=[B], writes=[B])
        S.add("act", lambda e: e.activation(tmp, tmp, AF.Ln), reads=[B], writes=[B])
        S.add("act", lambda e: e.activation(rstd, tmp, AF.Exp, scale=-0.5), reads=[B], writes=[B])

    def mm_feat(bank_ap, W, col0, n_m, rhs_of_c, start_first=True):
        def f(e):
            ins = None
            for c in range(16):
                ins = e.matmul(bank_ap, W[:, c, col0:col0 + n_m], rhs_of_c(c),
                               start=(c == 0 and start_first), stop=(c == 15), skip_group_check=True)
            return ins
        return f

    mark = apos[0]
    if True:
        Wkv = sb("Wkv", [128, 16, 2048], BF16)
        B_Wkv = Buf("Wkv")
        kst = [sb("kst%d" % k, [128, 8, 128], BF16) for k in range(2)]
        vst = [sb("vst%d" % k, [128, 1024], BF16) for k in range(2)]
        B_kst = [Buf("kst%d" % k) for k in range(2)]
        B_vst = [Buf("vst%d" % k) for k in range(2)]
        load_weights(Wkv, B_Wkv, [(1024, 3072)], npre_s)
        Wq = sb("Wq", [128, 16, 2048], BF16)
        B_Wq = Buf("Wq")
        qst = sb("qst", [128, 8, 128], BF16)
        B_qst = Buf("qst")
        ge = sb("ge", [128, 1024], F32)
        B_ge = Buf("ge")
        gst = sb("gst", [128, 1024], BF16)
        B_gst = Buf("gst")
        load_weights(Wq, B_Wq, [(0, 1024), (3072, 4096)], npre_s)
        norm_part(0, 0)
        norm_part(1, 1)
        transpose_part(0, (0, 6))
        for i in range(NB):
            k = i % 2
            if i + 1 < NB:
                transpose_part(i + 1, (0, 6))
            if i + 2 < NB:
                norm_part(i + 2, i)
            for pr in range(8):
                bk = 2 + pr // 4
                S.add("pe", mm_feat(psf(2, 1024)[:, pr * 128:(pr + 1) * 128], Wkv, pr * 128, 128,
                                    lambda c, k=k: uT[k][:, c, :], start_first=(pr % 4 == 0)),
                      reads=[B_Wkv, B_uT[k]], pwrites=[B_ps[bk]])
            S.add("act", lambda e, k=k: e.activation(kst[k][:].rearrange("p a t -> p (a t)"), psf(2, 1024), AF.Copy),
                  reads=[B_ps[2], B_ps[3]], writes=[B_kst[k]])
            S.add("pool", lambda e, k=k, i=i: e.dma_start(out=KT_d[:, i, :, :], in_=kst[k][:]),
                  reads=[B_kst[k]], writes=[B_KT[i]], key="kst%d" % k)
            for hf in range(2):
                def fv(e, hf=hf, k=k):
                    ins = None
                    for c in range(16):
                        ins = e.matmul(psf(4 + hf, 512), uT[k][:, c, :], Wkv[:, c, 1024 + hf * 512:1024 + (hf + 1) * 512],
                                       start=(c == 0), stop=(c == 15))
                    return ins
                S.add("pe", fv, reads=[B_Wkv, B_uT[k]], writes=[B_ps[4 + hf]])
            S.add("dve", lambda e, k=k: e.tensor_copy(vst[k][:], psf(4, 1024)),
                  reads=[B_ps[4], B_ps[5]], writes=[B_vst[k]])
            S.add("pool", lambda e, k=k, i=i: e.dma_start(out=V_d[:, i, :], in_=vst[k][:]),
                  reads=[B_vst[k]], writes=[B_V[i]], key="vst%d" % k)
            if i % 8 == 7:
                j = i // 8
                for pr in range(8):
                    bk = 2 + pr // 4
                    S.add("pe", mm_feat(psf(2, 1024)[:, pr * 128:(pr + 1) * 128], Wq, pr * 128, 128,
                                        lambda c, k=k: uT[k][:, c, :], start_first=(pr % 4 == 0)),
                          reads=[B_Wq, B_uT[k]], pwrites=[B_ps[bk]])
                S.add("act", lambda e: e.activation(qst[:].rearrange("p a t -> p (a t)"), psf(2, 1024), AF.Copy, scale=0.125),
                      reads=[B_ps[2], B_ps[3]], writes=[B_qst])
                S.add("pool", lambda e, j=j: e.dma_start(out=QT_d[:, j, :, :], in_=qst[:]),
                      reads=[B_qst], writes=[B_QT[j]], key="qst")
                for hf in range(2):
                    def fg(e, hf=hf, k=k):
                        ins = None
                        for c in range(16):
                            ins = e.matmul(psf(4 + hf, 512), uT[k][:, c, :], Wq[:, c, 1024 + hf * 512:1024 + (hf + 1) * 512],
                                           start=(c == 0), stop=(c == 15))
                        return ins
                    S.add("pe", fg, reads=[B_Wq, B_uT[k]], writes=[B_ps[4 + hf]])
                silu_from_psum(S, psf(4, 1024), [B_ps[4], B_ps[5]], ge, B_ge, gst, B_gst)
                S.add("pool", lambda e, j=j: e.dma_start(out=SG_d[:, j, :], in_=gst[:]),
                      reads=[B_gst], writes=[B_SG[j]], key="gst")
        p1_tail = [B_Wkv] + B_kst + B_vst + [B_Wq, B_qst, B_ge, B_gst]
    B_region = Buf("region")
    S.add("sp", lambda e: e.nop(), writes=p1_tail + [B_region])

    apos[0] = mark
    if True:
        W3 = sb("W3", [128, 16, 2560], BF16)
        B_W3 = Buf("W3")
        bias_s = sb("bias_s", [128, 16, 256], F32)
        mtmp = sb("mtmp", [128, 2, 256], F32)
        sink_s = sb("sink_s", [128, 16], F32)
        B_bias = Buf("bias")
        Ksw = sb("Ksw", [128, 4, 256], BF16)
        Vsw = sb("Vsw", [128, 2, 256], BF16)
        Qsw = sb("Qsw", [128, 8, 128], BF16)
        lg = sb("lg", [128, 4, 256], F32)
        pexp = sb("pexp", [128, 4, 256], BF16)
        pT = sb("pT", [128, 4, 2, 128], BF16)
        sm = sb("sm", [128, 32], F32)
        osw = sb("osw", [128, 1024], F32)
        ge3 = sb("ge3", [128, 1024], F32)
        sg3 = sb("sg3", [128, 1024], BF16)
        og3 = sb("og3", [128, 1024], BF16)
        B_Ksw, B_Vsw, B_Qsw, B_lg, B_pexp, B_pT, B_sm, B_osw, B_ge3, B_sg3, B_og3 = [
            Buf(n) for n in ("Ksw", "Vsw", "Qsw", "lg", "pexp", "pT", "sm", "osw", "ge3", "sg3", "og3")]
        S.add("sp", lambda e: e.nop(), reads=[B_region],
              writes=[B_W3, B_bias, B_Ksw, B_Vsw, B_Qsw, B_lg, B_pexp, B_pT, B_sm, B_osw, B_ge3, B_sg3, B_og3])
        load_weights(W3, B_W3, [(4096, 5120), (5120, 5376), (5376, 5632), (5632, 6656)], npre_s)
        W3kd = sb("W3kd", [128, 16, 4, 128], BF16)
        B_W3kd = Buf("W3kd")

        def fdup(e):
            ins = None
            for c in range(16):
                for half in range(2):
                    ins = e.tensor_copy(W3kd[:, c, :, half * 64:(half + 1) * 64],
                                        W3[:, c, 1024:1280].rearrange("p (g d) -> p g d", g=4))
            return ins
        S.add("pool", fdup, reads=[B_W3, B_region], writes=[B_W3kd])
        S.add("sp", lambda e: e.dma_start(out=bias_s[:].rearrange("p h c -> p (h c)"), in_=biasT), pwrites=[B_bias], key="bias")
        S.add("sp", lambda e: e.dma_start(out=mtmp[:, 0, :], in_=maskc), pwrites=[B_bias], key="bias")
        S.add("sp", lambda e: e.dma_start(out=mtmp[:, 1, :], in_=mask0), pwrites=[B_bias], key="bias")
        S.add("sp", lambda e: e.dma_start(out=sink_s[:], in_=sinkb), pwrites=[B_bias], key="bias")

        def bias_setup(e):
            ins = None
            for h in range(16):
                ins = e.tensor_tensor(bias_s[:, h, :], bias_s[:, h, :], mtmp[:, 0, :], ALU.add)
            return ins
        S.add("pool", bias_setup, reads=[B_bias], writes=[B_bias])

        norm_transpose(6, 0)
        norm_transpose(7, 1)
        for j in range(NS):
            for which, i in ((0, 8 * j + 6), (1, 8 * j + 7)):
                k = which
                for g in range(4):
                    def fk(e, g=g, k=k):
                        ins = None
                        for c in range(16):
                            ins = e.matmul(psf(2, 512)[:, g * 128:(g + 1) * 128], W3kd[:, c, g, :], uT[k][:, c, :],
                                           start=(c == 0 and g == 0), stop=(c == 15), skip_group_check=True)
                        return ins
                    S.add("pe", fk, reads=[B_W3kd, B_uT[k]], pwrites=[B_ps[2]])
                S.add("act", lambda e, which=which: e.activation(
                    Ksw[:, :, which * 128:(which + 1) * 128], psf(2, 512).rearrange("p (g t) -> p g t", g=4), AF.Copy),
                    reads=[B_ps[2]], pwrites=[B_Ksw])

                def fv3(e, k=k):
                    ins = None
                    for c in range(16):
                        ins = e.matmul(psf(3, 256), uT[k][:, c, :], W3[:, c, 1280:1536], start=(c == 0), stop=(c == 15))
                    return ins
                S.add("pe", fv3, reads=[B_W3, B_uT[k]], writes=[B_ps[3]])
                S.add("dve", lambda e, which=which: e.tensor_copy(Vsw[:, which, :], psf(3, 256)),
                      reads=[B_ps[3]], pwrites=[B_Vsw])
            for pr in range(8):
                bk = 4 + pr // 4
                S.add("pe", mm_feat(psf(4, 1024)[:, pr * 128:(pr + 1) * 128], W3, pr * 128, 128,
                                    lambda c, k=k: uT[k][:, c, :], start_first=(pr % 4 == 0)),
                      reads=[B_W3, B_uT[k]], pwrites=[B_ps[bk]])
            S.add("act", lambda e: e.activation(Qsw[:].rearrange("p a t -> p (a t)"), psf(4, 1024), AF.Copy, scale=0.125),
                  reads=[B_ps[4], B_ps[5]], writes=[B_Qsw])
            for hf in range(2):
                def fg3(e, hf=hf, k=k):
                    ins = None
                    for c in range(16):
                        ins = e.matmul(psf(6 + hf, 512), uT[k][:, c, :], W3[:, c, 1536 + hf * 512:1536 + (hf + 1) * 512],
                                       start=(c == 0), stop=(c == 15))
                    return ins
                S.add("pe", fg3, reads=[B_W3, B_uT[k]], writes=[B_ps[6 + hf]])
            silu_from_psum(S, psf(6, 1024), [B_ps[6], B_ps[7]], ge3, B_ge3, sg3, B_sg3)
            if j + 1 < NS:
                norm_transpose(8 * (j + 1) + 6, 0)
                norm_transpose(8 * (j + 1) + 7, 1)
            btab = bias_s
            PERM = (0, 2, 1, 3)
            for g in range(4):
                def flog(e, g=g):
                    ins = None
                    for s_ in range(4):
                        h = 4 * g + PERM[s_]
                        pb = (h % 2) * 64
                        ins = e.matmul(psf(2, 1024)[:, s_ * 256:(s_ + 1) * 256],
                                       Qsw[pb:pb + 64, h // 2, :], Ksw[pb:pb + 64, g, :],
                                       start=(s_ % 2 == 0), stop=True, skip_group_check=True)
                    return ins
                S.add("pe", flog, reads=[B_Qsw, B_Ksw], writes=[B_ps[2], B_ps[3]])
                S.add("dve", lambda e, g=g, btab=btab: e.tensor_tensor(
                    lg[:], psf(2, 1024).rearrange("p (h c) -> p h c", h=4), btab[:, 4 * g:4 * g + 4, :], ALU.add),
                    reads=[B_ps[2], B_ps[3], B_bias], writes=[B_lg])
                if j == 0:
                    def fm0(e):
                        ins = None
                        for s_ in range(4):
                            ins = e.tensor_tensor(lg[:, s_, :], lg[:, s_, :], mtmp[:, 1, :], ALU.add)
                        return ins
                    S.add("dve", fm0, reads=[B_bias], writes=[B_lg])
                S.add("dve", lambda e: e.tensor_reduce(sm[:, 0:4], lg[:], AX.X, ALU.max), reads=[B_lg], writes=[B_sm])
                S.add("dve", lambda e, g=g: e.tensor_tensor(sm[:, 0:4], sm[:, 0:4], sink_s[:, 4 * g:4 * g + 4], ALU.max),
                      reads=[B_sm, B_bias], writes=[B_sm])
                S.add("dve", lambda e: e.tensor_scalar(sm[:, 4:8], sm[:, 0:4], -1.0, None, ALU.mult),
                      reads=[B_sm], writes=[B_sm])
                S.add("dve", lambda e, g=g: e.tensor_tensor(sm[:, 12:16], sink_s[:, 4 * g:4 * g + 4], sm[:, 0:4], ALU.subtract),
                      reads=[B_sm, B_bias], writes=[B_sm])

                def fexp(e):
                    ins = None
                    for hh in range(4):
                        ins = e.activation(pexp[:, hh, :], lg[:, hh, :], AF.Exp, bias=sm[:, 4 + hh:5 + hh],
                                           accum_out=sm[:, 8 + hh:9 + hh])
                    ins = e.activation(sm[:, 12:16], sm[:, 12:16], AF.Exp)
                    return ins
                S.add("act", fexp, reads=[B_lg, B_sm], writes=[B_pexp, B_sm])
                S.add("dve", lambda e: e.tensor_tensor(sm[:, 8:12], sm[:, 8:12], sm[:, 12:16], ALU.add),
                      reads=[B_sm], writes=[B_sm])
                S.add("dve", lambda e: e.reciprocal(sm[:, 16:20], sm[:, 8:12]), reads=[B_sm], writes=[B_sm])

                def ftr(e):
                    ins = None
                    for hh in range(4):
                        for half in range(2):
                            ins = e.transpose(psb(0, 1024)[:, (hh * 2 + half) * 128:(hh * 2 + half + 1) * 128],
                                              pexp[:, hh, half * 128:(half + 1) * 128], ident[:])
                    return ins
                S.add("pe", ftr, reads=[B_pexp, B_const], writes=[B_ps[0]])
                S.add("act", lambda e: e.activation(pT[:].rearrange("p h a t -> p (h a t)"), psb(0, 1024), AF.Copy),
                      reads=[B_ps[0]], writes=[B_pT])

                def fpv(e, g=g):
                    ins = None
                    for hh in range(4):
                        for half in range(2):
                            ins = e.matmul(psf(1, 256)[:, hh * 64:(hh + 1) * 64], pT[:, hh, half, :],
                                           Vsw[:, half, g * 64:(g + 1) * 64],
                                           start=(hh == 0 and half == 0), stop=(half == 1), skip_group_check=True)
                    return ins
                S.add("pe", fpv, reads=[B_pT, B_Vsw], writes=[B_ps[1]])

                def fnorm(e, g=g):
                    ins = None
                    for s_ in range(4):
                        h = 4 * g + PERM[s_]
                        ins = e.tensor_scalar(osw[:, h * 64:(h + 1) * 64], psf(1, 256)[:, s_ * 64:(s_ + 1) * 64],
                                              sm[:, 16 + s_:17 + s_], None, ALU.mult)
                    return ins
                S.add("dve", fnorm, reads=[B_ps[1], B_sm], pwrites=[B_osw])
            S.add("act", lambda e: e.activation(ge3[:], osw[:], AF.Square, accum_out=sm[:, 20:21]),
                  reads=[B_osw], writes=[B_ge3, B_sm])
            rstd_chain(sm[:, 20:21], sm[:, 21:22], sm[:, 22:23], 1.0 / 1024, B_sm)
            S.add("dve", lambda e: e.scalar_tensor_tensor(og3[:], osw[:], sm[:, 22:23], sg3[:], ALU.mult, ALU.mult),
                  reads=[B_osw, B_sm, B_sg3], writes=[B_og3])
            S.add("pool", lambda e, j=j: e.dma_start(out=OGW_d[:, j, :], in_=og3[:]),
                  reads=[B_og3], writes=[B_OGW[j]], key="og3")
        p3_tail = [B_W3, B_W3kd, B_bias, B_Ksw, B_Vsw, B_Qsw, B_lg, B_pexp, B_pT, B_sm, B_osw, B_ge3, B_sg3, B_og3]
    S.add("sp", lambda e: e.nop(), writes=p3_tail + B_xt + [B_xjunk, B_stat] + B_xb + B_uT + B_wst + [B_region])

    CH = 2
    apos[0] = mark
    if True:
        sb4 = sb
        Wo = sb4("Wo", [128, 16, 2048], BF16)
        B_Wo = Buf("Wo")
        npost_s = sb4("npost_s", [128, D], F32)
        KA = [sb4("KA%d" % k, [128, CH, 8, 128], BF16) for k in range(2)]
        KB = [sb4("KB%d" % k, [128, CH, 8, 128], BF16) for k in range(2)]
        VC = [sb4("VC%d" % k, [128, CH, 1024], BF16) for k in range(2)]
        B_KA = [Buf("KA%d" % k) for k in range(2)]
        B_KB = [Buf("KB%d" % k) for k in range(2)]
        B_VC = [Buf("VC%d" % k) for k in range(2)]
        QA = sb4("QA", [128, 8, 128], BF16)
        QB = sb4("QB", [128, 8, 128], BF16)
        B_QAd, B_QAr, B_QBd, B_QBr = Buf("QAd"), Buf("QAr"), Buf("QBd"), Buf("QBr")
        eb = [sb4("eb%d" % x, [128, 1024], F32) for x in range(2)]
        spb = [sb4("spb%d" % x, [128, 1024], BF16) for x in range(2)]
        Ab = [sb4("Ab%d" % x, [128, 1024], BF16) for x in range(2)]
        B_eb = [Buf("eb%d" % x) for x in range(2)]
        B_spb = [Buf("spb%d" % x) for x in range(2)]
        B_Ab = [Buf("Ab%d" % x) for x in range(2)]
        osb = sb4("osb", [128, 8, 2, 64], F32)
        B_osb = Buf("osb")
        og = sb4("og", [128, 2048], BF16)
        B_og = Buf("og")
        sgl = sb4("sgl", [128, 1024], BF16)
        B_sgl = Buf("sgl")
        ogT = uT[0]
        B_ogT = Buf("ogT")
        xo = xt[0]
        B_xo = Buf("xo")
        yb = xt[1]
        B_yb = Buf("yb")
        jk = xjunk
        B_jk = Buf("jk")
        st4 = sb4("st4", [128, 8], F32)
        B_st4 = Buf("st4")
        all4 = [B_Wo] + B_KA + B_KB + B_VC + [B_QAd, B_QAr, B_QBd, B_QBr] + B_eb + B_spb + B_Ab + \
            [B_osb, B_og, B_sgl, B_ogT, B_xo, B_yb, B_jk, B_st4]
        S.add("sp", lambda e: e.nop(), reads=[B_region], writes=all4)
        wst4 = wst
        B_wst4 = [Buf("wst4_%d" % k) for k in range(2)]
        for c in range(16):
            k = c % 2
            S.add("sp", lambda e, k=k, c=c: e.dma_start(out=wst4[k][:], in_=w_out[c * 128:(c + 1) * 128, :]),
                  writes=[B_wst4[k]], key="wst4_%d" % k)
            S.add("dve", lambda e, k=k, c=c: e.tensor_scalar(Wo[:, c, :], wst4[k][:], gsbw_s[:, c:c + 1], None, ALU.mult),
                  reads=[B_wst4[k], B_const], pwrites=[B_Wo])
        S.add("sp", lambda e: e.dma_start(out=npost_s[:], in_=npost), pwrites=[B_Wo], key="npost")

        Rq = [sb4("Rq%d" % x, [128, 1024], BF16) for x in range(2)]
        sel = sb4("sel", [128, 2, 128], BF16)

        def kconst(e):
            ins = None
            for k in range(2):
                e.memset(KA[k][64:128], 0.0)
                e.memset(KB[k][0:64], 0.0)
            e.memset(QA[64:128], 0.0)
            e.memset(QB[0:64], 0.0)
            for x in range(2):
                ins = e.memset(Rq[x][:], 0.0)
            return ins
        S.add("pool", kconst, writes=B_KA + B_KB + [B_QAr, B_QBr, B_QAd, B_QBd])
        B_sel = Buf("sel")
        S.add("sp", lambda e: e.dma_start(out=sel[:].rearrange("p x m -> p (x m)"), in_=sel_d),
              reads=[B_region], writes=[B_sel], key="sel")

        ZB = [[B_ps[0], B_ps[1]], [B_ps[2], B_ps[3]]]
        RB = [B_ps[4], B_ps[5]]
        B_R = [Buf("R_A"), Buf("R_B")]
        OB = [B_ps[6], B_ps[7]]
        Qt = [QA, QB]
        Kt = [KA, KB]
        KR = [65, 128]
        AUG = [64, 32]
        B_Qd = [B_QAd, B_QBd]
        B_Qr = [B_QAr, B_QBr]
        B_Kt = [B_KA, B_KB]
        nchunk = 0

        def z_step(X, kb, bi, first):

            def f(e):
                ins = None
                for h in range(8):
                    ins = e.matmul(psf(2 * X, 1024)[:, h * 128:(h + 1) * 128], Kt[X][kb][:, bi, h, :],
                                   Qt[X][:, h, :],
                                   start=(h % 4 == 0), stop=False, skip_group_check=True)
                if first:
                    for k2 in range(2):
                        ins = e.matmul(psf(2 * X + k2, 512), ident[:], dmask[:, k2 * 512:(k2 + 1) * 512],
                                       start=False, stop=False, skip_group_check=True)
                return ins
            S.add("pe", f, reads=[B_Kt[X][kb], B_Qd[X], B_const], writes=ZB[X])

        for j in range(NS):
            nblk = 8 * j + 8
            S.add("sp", lambda e, j=j: e.dma_start(out=QA[0:64], in_=QT_d[0:64, j, :, :]),
                  reads=[B_QT[j]], writes=[B_QAd], key="QAd")
            S.add("sp", lambda e, j=j: e.dma_start(out=QB[64:128], in_=QT_d[64:128, j, :, :]),
                  reads=[B_QT[j]], writes=[B_QBd], key="QBd")
            blocks = list(range(nblk - 1, -1, -1))
            chunks = [blocks[a:a + CH] for a in range(0, nblk, CH)]

            def load_chunk(ch, kb):
                lo = ch[-1]
                n = len(ch)
                S.add("sp", lambda e: e.dma_start(out=KA[kb][0:64, 0:n], in_=KT_d[0:64, lo:lo + n, :, :]),
                      reads=[B_KT[b] for b in ch], writes=[B_KA[kb]], key="KA%d" % kb)
                S.add("sp", lambda e: e.dma_start(out=KB[kb][64:128, 0:n], in_=KT_d[64:128, lo:lo + n, :, :]),
                      reads=[B_KT[b] for b in ch], writes=[B_KB[kb]], key="KB%d" % kb)
                S.add("sp", lambda e: e.dma_start(out=VC[kb][:, 0:n], in_=V_d[:, lo:lo + n, :]),
                      reads=[B_V[b] for b in ch], writes=[B_VC[kb]], key="VC%d" % kb)

            kb0 = nchunk % 2
            load_chunk(chunks[0], kb0)
            seq = [(ci, b) for ci, ch in enumerate(chunks) for b in ch]
            for X in range(2):
                z_step(X, kb0, seq[0][1] - chunks[0][-1], True)
            for si, (ci, b) in enumerate(seq):
                kb = (nchunk + ci) % 2
                bi = b - chunks[ci][-1]
                first = (si == 0)
                last = (si == len(seq) - 1)
                if bi == len(chunks[ci]) - 1 and ci + 1 < len(chunks):
                    load_chunk(chunks[ci + 1], (nchunk + ci + 1) % 2)
                for X in range(2):
                    S.add("act", lambda e, X=X: e.activation(eb[X][:], psf(2 * X, 1024), AF.Exp),
                          reads=ZB[X], writes=[B_eb[X]])
                if not first:
                    for X in range(2):
                        def fradd(e, X=X):
                            ins = None
                            for k2 in range(2):
                                ins = e.matmul(psf(2 * X + k2, 512), sel[:, X, :], Rq[X][:, k2 * 512:(k2 + 1) * 512],
                                               start=False, stop=False, skip_group_check=True)
                            return ins
                        S.add("pe", fradd, reads=[B_const, B_Qr[X], B_sel], writes=ZB[X])
                for X in range(2):
                    S.add("act", lambda e, X=X: e.activation(spb[X][:], eb[X][:], AF.Ln, bias=1.0),
                          reads=[B_eb[X]], writes=[B_spb[X]])
                for X in range(2):
                    def ftri(e, X=X):
                        ins = None
                        for k2 in range(2):
                            ins = e.matmul(psf(2 * X + k2, 512), tri[:], spb[X][:, k2 * 512:(k2 + 1) * 512],
                                           start=False, stop=True, skip_group_check=True)
                        return ins
                    S.add("pe", ftri, reads=[B_spb[X], B_const], writes=ZB[X])
                    if not last:
                        def ferow(e, X=X, first=first):
                            ins = None
                            for k2 in range(2):
                                mx = 65 if X == 0 else 33
                                ins = e.matmul(psf(4 + k2, 512)[0:mx, :], erow[:, 128 * X:128 * X + mx],
                                               spb[X][:, k2 * 512:(k2 + 1) * 512],
                                               start=(first and X == 0), stop=False, skip_group_check=True)
                            return ins
                        S.add("pe", ferow, reads=[B_spb[X], B_const], writes=[B_R[X]])
                        a = AUG[X]
                        S.add("dve", lambda e, X=X, a=a: e.tensor_copy(
                            Rq[X][a:a + 1, :], psf(4, 1024)[a:a + 1, :]),
                            reads=[B_R[X]], writes=[B_Qr[X]])
                for X in range(2):
                    S.add("act", lambda e, X=X: e.activation(Ab[X][:], psf(2 * X, 1024), AF.Exp),
                          reads=ZB[X], writes=[B_Ab[X]])
                for X in range(2):
                    if not last:
                        ci2, b2 = seq[si + 1]
                        z_step(X, (nchunk + ci2) % 2, b2 - chunks[ci2][-1], False)

                    def fav(e, X=X, kb=kb, bi=bi, first=first, last=last):
                        ins = None
                        for h in range(8):
                            hd = 2 * h + X
                            ins = e.matmul(psf(6 + X, 512)[:, h * 64:(h + 1) * 64], Ab[X][:, h * 128:(h + 1) * 128],
                                           VC[kb][:, bi, hd * 64:(hd + 1) * 64],
                                           start=(first and h == 0), stop=last, skip_group_check=True)
                        return ins
                    S.add("pe", fav, reads=[B_Ab[X], B_VC[kb]], writes=[OB[X]])
            nchunk += len(chunks)
            for X in range(2):
                S.add("dve", lambda e, X=X: e.tensor_copy(osb[:, :, X, :], psf(6 + X, 512).rearrange("p (h d) -> p h d", h=8)),
                      reads=[OB[X]], pwrites=[B_osb])
            osb2 = osb[:].rearrange("p h x d -> p (h x d)")
            S.add("sp", lambda e, j=j: e.dma_start(out=sgl[:], in_=SG_d[:, j, :]), reads=[B_SG[j]], writes=[B_sgl], key="sgl")
            S.add("sp", lambda e, j=j: e.dma_start(out=og[:, 1024:2048], in_=OGW_d[:, j, :]), reads=[B_OGW[j]],
                  pwrites=[B_og], key="og")
            S.add("sp", lambda e, j=j: e.dma_start(out=xo[:], in_=xs[(8 * j + 7) * 128:(8 * j + 8) * 128, :]),
                  writes=[B_xo], key="xo")
            S.add("act", lambda e: e.activation(jk[:, 0:1024], osb2, AF.Square, accum_out=st4[:, 0:1]),
                  reads=[B_osb], writes=[B_jk, B_st4])
            rstd_chain(st4[:, 0:1], st4[:, 1:2], st4[:, 2:3], 1.0 / 1024, B_st4)
            S.add("dve", lambda e: e.scalar_tensor_tensor(og[:, 0:1024], osb2, st4[:, 2:3], sgl[:], ALU.mult, ALU.mult),
                  reads=[B_osb, B_st4, B_sgl], pwrites=[B_og])

            def ftr4(e):
                ins = None
                for c in range(16):
                    ins = e.transpose(psb(0, 2048)[:, c * 128:(c + 1) * 128], og[:, c * 128:(c + 1) * 128], ident[:])
                return ins
            S.add("pe", ftr4, reads=[B_og, B_const], writes=[B_ps[0], B_ps[1]])
            S.add("act", lambda e: e.activation(ogT[:].rearrange("p c t -> p (c t)"), psb(0, 2048), AF.Copy),
                  reads=[B_ps[0], B_ps[1]], writes=[B_ogT])
            for q4 in range(4):
                def fy(e, q4=q4):
                    ins = None
                    for c in range(16):
                        ins = e.matmul(psf(q4, 512), ogT[:, c, :], Wo[:, c, q4 * 512:(q4 + 1) * 512],
                                       start=(c == 0), stop=(c == 15))
                    return ins
                S.add("pe", fy, reads=[B_ogT, B_Wo], writes=[B_ps[q4]])
            S.add("act", lambda e: e.activation(yb[:], psf(0, 2048), AF.Square, accum_out=st4[:, 3:4]),
                  reads=[B_ps[0], B_ps[1], B_ps[2], B_ps[3]], writes=[B_yb, B_st4])
            rstd_chain(st4[:, 3:4], st4[:, 4:5], st4[:, 5:6], 1.0 / D, B_st4)
            S.add("dve", lambda e: e.scalar_tensor_tensor(yb[:], psf(0, 2048), st4[:, 5:6], npost_s[:], ALU.mult, ALU.mult),
                  reads=[B_ps[0], B_ps[1], B_ps[2], B_ps[3], B_st4, B_Wo], writes=[B_yb])
            S.add("pool", lambda e: e.tensor_tensor(yb[:], yb[:], xo[:], ALU.add), reads=[B_xo], writes=[B_yb])
            S.add("sp", lambda e, j=j: e.dma_start(out=out[j * 128:(j + 1) * 128, :], in_=yb[:]),
                  reads=[B_yb], writes=[Buf("out%d" % j)], key="yb")
        S.add("sp", lambda e: e.nop(), writes=all4 + B_wst4 + [Buf("end")])
        S.emit(nc, st)
    st.close()
    return nc


def silu_from_psum(S, z_ps, B_z, ge, B_ge, gout, B_gout):
    S.add("act", lambda e: e.activation(ge[:], z_ps, AF.Exp, scale=-1.0), reads=B_z, writes=[B_ge])
    S.add("dve", lambda e: e.tensor_scalar(ge[:], ge[:], 1.0, None, ALU.add), reads=[B_ge], writes=[B_ge])
    S.add("dve", lambda e: e.reciprocal(ge[:], ge[:]), reads=[B_ge], writes=[B_ge])
    S.add("dve", lambda e: e.tensor_tensor(gout[:], z_ps, ge[:], ALU.mult), reads=B_z + [B_ge], writes=[B_gout])


def _t5_bucket_table():
    qi = np.arange(128)[:, None]
    ci = np.arange(256)[None, :]
    dist = qi + 128 - ci
    n = np.maximum(dist, 0)
    nf = np.maximum(n, 1).astype(np.float32)
    large = 16 + (np.log(nf / np.float32(16)) / np.float32(np.log(128 / 16)) * np.float32(16)).astype(np.int32)
    large = np.minimum(large, 31)
    bucket = np.where(n < 16, n, large)
    valid = (dist >= 0) & (dist < 128)
    return bucket, valid


def host_inputs(x, w_in, w_out, norm_pre, norm_post, gn_sb, gn_sw, sinks, rel_bias, NS):
    bf = ml_dtypes.bfloat16
    NB = 8 * NS
    x2 = np.ascontiguousarray(x.reshape(-1, D)).astype(np.float32, copy=False)
    ntile = x2.shape[0] // 128
    assert ntile == NB
    bucket, valid = _t5_bucket_table()
    bias = rel_bias.astype(np.float32)[bucket]
    hord = [4 * g + p for g in range(4) for p in (0, 2, 1, 3)]
    biasT = np.ascontiguousarray(bias[:, :, hord].transpose(0, 2, 1)).reshape(128, 16 * 256)
    maskc = np.where(valid, 0.0, NEG).astype(np.float32)
    ident = np.eye(128, dtype=np.float32).astype(bf)
    jj = np.arange(128)[:, None]
    ss = np.arange(128)[None, :]
    tri = np.where(jj >= ss, -1.0, 0.0).astype(np.float32).astype(bf)
    erow = np.zeros((128, 256), np.float32)
    erow[:, 64] = -1.0
    erow[:, 128 + 32] = -1.0
    erow = erow.astype(bf)
    dm = np.where(jj < ss, 0.0, NEG).astype(np.float32)
    dmask = np.tile(dm, (1, 8)).astype(bf)
    selm = np.zeros((128, 256), np.float32)
    selm[64, 0:128] = 1.0
    selm[32, 128:256] = 1.0
    common = dict(
        w_in=np.ascontiguousarray(w_in.reshape(D, DIN)), w_out=np.ascontiguousarray(w_out.reshape(D, D)),
        npre=np.ascontiguousarray(norm_pre.reshape(16, 128).T),
        gsbw=np.ascontiguousarray(np.concatenate([gn_sb.reshape(8, 128), gn_sw.reshape(8, 128)], 0).T),
        npost=np.ascontiguousarray(np.broadcast_to(norm_post.reshape(1, D), (128, D))),
        sinkb=np.ascontiguousarray(np.broadcast_to(sinks.reshape(16)[hord].reshape(1, 16), (128, 16))),
        biasT=biasT, maskc=maskc, ident=ident, tri=tri, erow=erow, dmask=dmask, sel=selm.astype(bf))
    in_maps = []
    for c in range(NCORES):
        pad = 7 - c
        xs = np.zeros((NB * 128, D), np.float32)
        n_real = (NB - pad) * 128
        xs[pad * 128:] = x2[:n_real]
        m0 = np.zeros((128, 256), np.float32)
        if c == 0:
            m0[:, :128] = NEG
        d = dict(common)
        d["xs"] = xs
        d["mask0"] = m0
        in_maps.append(d)
    return in_maps


def run(inputs, NS):
    x = np.asarray(inputs["x"])
    in_maps = host_inputs(x, *[np.asarray(inputs[k], dtype=np.float32) for k in
                               ("w_in", "w_out", "norm_pre", "norm_post", "gn_sb", "gn_sw", "sinks", "rel_bias")], NS=NS)
    nc = build_nc(NS)
    res = run_bass_kernel_spmd(nc, in_maps, core_ids=list(range(NCORES)))
    outs = [np.asarray(r["out"]).reshape(NS, 128, D) for r in res.results]
    full = np.stack(outs, axis=1)
    return full.reshape(1, NS * 8 * 128, D).astype(np.float32)


def kernel(x, w_in, w_out, norm_pre, norm_post, gn_sb, gn_sw, sinks, rel_bias):
    return run(dict(x=x, w_in=w_in, w_out=w_out, norm_pre=norm_pre, norm_post=norm_post, gn_sb=gn_sb,
                    gn_sw=gn_sw, sinks=sinks, rel_bias=rel_bias), NS=16)
```
